# Optimizing a Trainium2 kernel written in Bass

```python
import jax, jax.numpy as jnp
from jax import lax
import numpy as np

D_MODEL = 1024
BATCH = 8
SEQ = 8192
DEPTH = 1

MIX_WIDTH = D_MODEL
RWKV_WIDTH = MIX_WIDTH // 2
POOL_WIDTH = MIX_WIDTH - RWKV_WIDTH
HEAD_SIZE = 64
N_HEADS = RWKV_WIDTH // HEAD_SIZE
D_DECAY_LORA = max(32, int(round(1.8 * RWKV_WIDTH ** 0.5 / 32)) * 32)
D_AAA_LORA = max(32, int(round(1.8 * RWKV_WIDTH ** 0.5 / 32)) * 32)
D_GATE_LORA = max(32, int(round(0.6 * RWKV_WIDTH ** 0.8 / 32)) * 32)
POOL_WINDOWS = (2, 4, 8, 16)
N_POOL_GROUPS = len(POOL_WINDOWS)
POOL_GROUP = POOL_WIDTH // N_POOL_GROUPS
D_FF = 256 * ((8 * D_MODEL // 3 + 255) // 256)
SPLIT_SIZES = (RWKV_WIDTH, RWKV_WIDTH, RWKV_WIDTH, D_DECAY_LORA, D_AAA_LORA, D_GATE_LORA, POOL_WIDTH)
RWKV_COLS = sum(SPLIT_SIZES[:6])
IN_COLS = sum(SPLIT_SIZES)
RMS_EPS = 1e-6
GN_EPS = 64e-5
L2_EPS = 1e-12

kernel_name = "hymba_rwkv7_multiscale_pool_macaron"


def rms_norm(x, g):
    xf = x.astype(jnp.float32)
    y = xf * lax.rsqrt(jnp.mean(xf * xf, axis=-1, keepdims=True) + RMS_EPS)
    return (y * g).astype(x.dtype)


def swiglu(x, w_gate, w_up, w_down):
    return (jax.nn.silu(x @ w_gate) * (x @ w_up)) @ w_down


def token_shift(p):
    return jnp.pad(p, ((0, 0), (1, 0), (0, 0)))[:, :-1]


def rwkv7_time_mix(p_r, p_k, p_v, p_w, p_a, p_g, w0, w_lora_up, a0, a_lora_up, g_lora_up,
                   k_k, k_a, r_k, ln_w, ln_b):
    B, S, _ = p_r.shape
    f32 = jnp.float32
    log_w = -jax.nn.softplus(-(w0 + jnp.tanh(p_w) @ w_lora_up).astype(f32)) - 0.5
    decay = jnp.exp(-jnp.exp(log_w))
    a = jax.nn.sigmoid((a0 + p_a @ a_lora_up).astype(f32))
    g = jax.nn.sigmoid(p_g) @ g_lora_up

    def heads(t):
        return t.reshape(B, S, N_HEADS, HEAD_SIZE).astype(f32)

    kk = heads(p_k * k_k)
    kk = kk / jnp.maximum(jnp.sqrt(jnp.sum(kk * kk, axis=-1, keepdims=True)), L2_EPS)
    k = heads(p_k * (1.0 + (a - 1.0) * k_a))
    r, v = heads(p_r), heads(p_v)
    w_h, a_h = heads(decay), heads(a)

    def step(state, inp):
        r_t, w_t, k_t, v_t, kk_t, a_t = inp
        sa = jnp.einsum('bhvk,bhk->bhv', state, -kk_t)
        state = (state * w_t[:, :, None, :]
                 + sa[..., None] * (kk_t * a_t)[:, :, None, :]
                 + v_t[..., None] * k_t[:, :, None, :])
        return state, jnp.einsum('bhvk,bhk->bhv', state, r_t)

    seq_major = lambda t: jnp.moveaxis(t, 1, 0)
    state0 = jnp.zeros((B, N_HEADS, HEAD_SIZE, HEAD_SIZE), f32)
    _, y = lax.scan(step, state0, (seq_major(r), seq_major(w_h), seq_major(k),
                                   seq_major(v), seq_major(kk), seq_major(a_h)))
    y = jnp.moveaxis(y, 0, 1)
    mean = jnp.mean(y, axis=-1, keepdims=True)
    var = jnp.mean(jnp.square(y - mean), axis=-1, keepdims=True)
    y = ((y - mean) * lax.rsqrt(var + GN_EPS)).reshape(B, S, RWKV_WIDTH) * ln_w + ln_b
    bonus = jnp.sum(r * k * r_k, axis=-1, keepdims=True) * v
    y = y + bonus.reshape(B, S, RWKV_WIDTH)
    return (y * g).astype(p_r.dtype)


def multiscale_pool_mix(p, w_pool, pool_scale):
    B, S, _ = p.shape
    f32 = jnp.float32
    pg = p.reshape(B, S, N_POOL_GROUPS, POOL_GROUP).astype(f32)
    cs = jnp.pad(jnp.cumsum(pg, axis=1), ((0, 0), (1, 0), (0, 0), (0, 0)))
    t1 = jnp.arange(1, S + 1)
    outs = []
    for gi, win in enumerate(POOL_WINDOWS):
        csg = cs[:, :, gi]
        lo = jnp.maximum(t1 - win, 0)
        window_sum = csg[:, 1:] - jnp.take(csg, lo, axis=1)
        count = jnp.minimum(t1, win).astype(f32)[None, :, None]
        outs.append(window_sum / count - pg[:, :, gi])
    pooled = jnp.stack(outs, axis=2)
    mixed = jnp.einsum('bsgc,gcd->bsgd', pooled, w_pool.astype(f32))
    return (mixed.reshape(B, S, POOL_WIDTH) * pool_scale).astype(p.dtype)


def setup_inputs(seed: int = 0) -> dict:
    key = jax.random.key(seed)
    ks = jax.random.split(key, 32)
    L = DEPTH
    f32 = jnp.float32

    def nrm(k, shape, scale):
        return jax.random.normal(k, shape, f32) * scale

    return {
        "x": nrm(ks[0], (BATCH, SEQ, D_MODEL), 1.0),
        "ffn1_norm": 1.0 + nrm(ks[1], (L, D_MODEL), 0.1),
        "ffn1_w_gate": nrm(ks[2], (L, D_MODEL, D_FF), D_MODEL ** -0.5),
        "ffn1_w_up": nrm(ks[3], (L, D_MODEL, D_FF), D_MODEL ** -0.5),
        "ffn1_w_down": nrm(ks[4], (L, D_FF, D_MODEL), D_FF ** -0.5),
        "mix_norm": 1.0 + nrm(ks[5], (L, D_MODEL), 0.1),
        "w_in": nrm(ks[6], (L, D_MODEL, IN_COLS), D_MODEL ** -0.5),
        "mu_shift": jax.random.uniform(ks[7], (L, RWKV_COLS), f32),
        "w0": jax.random.uniform(ks[8], (L, RWKV_WIDTH), f32, -6.0, -1.0),
        "w_lora_up": nrm(ks[9], (L, D_DECAY_LORA, RWKV_WIDTH), 0.5 * D_DECAY_LORA ** -0.5),
        "a0": nrm(ks[10], (L, RWKV_WIDTH), 0.1),
        "a_lora_up": nrm(ks[11], (L, D_AAA_LORA, RWKV_WIDTH), D_AAA_LORA ** -0.5),
        "g_lora_up": nrm(ks[12], (L, D_GATE_LORA, RWKV_WIDTH), D_GATE_LORA ** -0.5),
        "k_k": 0.85 + nrm(ks[13], (L, RWKV_WIDTH), 0.1),
        "k_a": 1.0 + nrm(ks[14], (L, RWKV_WIDTH), 0.1),
        "r_k": nrm(ks[15], (L, N_HEADS, HEAD_SIZE), 0.1),
        "ln_w": 1.0 + nrm(ks[16], (L, RWKV_WIDTH), 0.1),
        "ln_b": nrm(ks[17], (L, RWKV_WIDTH), 0.02),
        "w_pool": nrm(ks[18], (L, N_POOL_GROUPS, POOL_GROUP, POOL_GROUP), POOL_GROUP ** -0.5),
        "pool_scale": 1.0 + nrm(ks[19], (L, POOL_WIDTH), 0.1),
        "w_out": nrm(ks[20], (L, MIX_WIDTH, D_MODEL), MIX_WIDTH ** -0.5),
        "ffn2_norm": 1.0 + nrm(ks[21], (L, D_MODEL), 0.1),
        "ffn2_w_gate": nrm(ks[22], (L, D_MODEL, D_FF), D_MODEL ** -0.5),
        "ffn2_w_up": nrm(ks[23], (L, D_MODEL, D_FF), D_MODEL ** -0.5),
        "ffn2_w_down": nrm(ks[24], (L, D_FF, D_MODEL), D_FF ** -0.5),
        "final_norm": 1.0 + nrm(ks[25], (D_MODEL,), 0.1),
    }


def reference(x, ffn1_norm, ffn1_w_gate, ffn1_w_up, ffn1_w_down, mix_norm, w_in, mu_shift,
              w0, w_lora_up, a0, a_lora_up, g_lora_up, k_k, k_a, r_k, ln_w, ln_b,
              w_pool, pool_scale, w_out, ffn2_norm, ffn2_w_gate, ffn2_w_up, ffn2_w_down,
              final_norm):
    split_points = np.cumsum(SPLIT_SIZES[:5]).tolist()
    for l in range(DEPTH):
        x = x + 0.5 * swiglu(rms_norm(x, ffn1_norm[l]), ffn1_w_gate[l], ffn1_w_up[l], ffn1_w_down[l])

        h = rms_norm(x, mix_norm[l])
        p = h @ w_in[l]
        p_rw, p_pool = p[..., :RWKV_COLS], p[..., RWKV_COLS:]
        p_rw = p_rw + (token_shift(p_rw) - p_rw) * mu_shift[l]
        p_r, p_k, p_v, p_w, p_a, p_g = jnp.split(p_rw, split_points, axis=-1)
        y_rw = rwkv7_time_mix(p_r, p_k, p_v, p_w, p_a, p_g, w0[l], w_lora_up[l], a0[l],
                              a_lora_up[l], g_lora_up[l], k_k[l], k_a[l], r_k[l],
                              ln_w[l], ln_b[l])
        y_pool = multiscale_pool_mix(p_pool, w_pool[l], pool_scale[l])
        x = x + jnp.concatenate([y_rw, y_pool], axis=-1) @ w_out[l]

        x = x + 0.5 * swiglu(rms_norm(x, ffn2_norm[l]), ffn2_w_gate[l], ffn2_w_up[l], ffn2_w_down[l])
    return rms_norm(x, final_norm)
```

```python
import os
import contextlib
import numpy as np
import concourse.bass as bass
import concourse.mybir as mybir
from concourse.bass_utils import run_bass_kernel_spmd

F32 = mybir.dt.float32
BF16 = mybir.dt.bfloat16
ALU = mybir.AluOpType
AF = mybir.ActivationFunctionType
AX = mybir.AxisListType

D = 1024
SEQ = 8192
DFF = 2816
NF = DFF // 128
TT = 512
C = 64
NCH = TT // C
INC = 2208
C0 = float(np.exp(-0.5))
RMS_EPS = 1e-6
GN_EPS = 64e-5


class Buf:
    __slots__ = ("name", "w", "r", "psum")

    def __init__(self, name, psum=False):
        self.name = name
        self.w = None
        self.r = []
        self.psum = psum


class Op:
    __slots__ = ("eng", "fn", "deps", "idx", "inc", "semval", "dma", "ndma")

    def __init__(self, eng, fn, dma=None, ndma=1):
        self.eng = eng
        self.fn = fn
        self.deps = set()
        self.inc = False
        self.semval = None
        self.dma = dma
        self.ndma = ndma


class Sched:
    ENGS = ("pe", "act", "dve", "pool", "sp")

    def __init__(self, nc):
        self.nc = nc
        self.ops = []

    def add(self, eng, fn, reads=(), writes=(), dma=None, ndma=1):
        op = Op(eng, fn, dma, ndma)
        op.idx = len(self.ops)
        xr = [b for b in reads if b.psum]
        if xr:
            reads = [b for b in reads if not b.psum]
            writes = list(writes) + xr
        for b in reads:
            if b.w is not None:
                op.deps.add(b.w)
        for b in writes:
            if b.w is not None:
                op.deps.add(b.w)
            for r in b.r:
                op.deps.add(r)
        for b in reads:
            b.r.append(op.idx)
        for b in writes:
            b.w = op.idx
            b.r = []
        op.deps.discard(op.idx)
        self.ops.append(op)
        return op

    def emit(self, final_waits=()):
        nc = self.nc
        ops = self.ops
        pos = {}
        cnt = {}
        dma_groups = []
        for op in ops:
            if op.dma is not None:
                key = ("dma", op.dma)
                if key not in cnt:
                    cnt[key] = 0
                    dma_groups.append(op.dma)
                cnt[key] += 16 * op.ndma
            else:
                key = ("eng", op.eng)
                cnt[key] = cnt.get(key, 0) + 1
            pos[op.idx] = (key, cnt[key])
        know = {e: {} for e in self.ENGS}
        front = [None] * len(ops)
        wdeps = [None] * len(ops)
        for op in ops:
            K = know[op.eng]
            need = {}
            for d in op.deps:
                key, p = pos[d]
                if need.get(key, (0, None))[0] < p:
                    need[key] = (p, d)
            wl = []
            for key, (p, d) in sorted(need.items(), key=lambda kv: -kv[1][1]):
                if K.get(key, 0) < p:
                    wl.append(d)
                    ops[d].inc = True
                    for k2, v2 in front[d].items():
                        if K.get(k2, 0) < v2:
                            K[k2] = v2
            wdeps[op.idx] = wl
            f = dict(K)
            k0, v0 = pos[op.idx]
            if f.get(k0, 0) < v0:
                f[k0] = v0
            front[op.idx] = f
        for i in final_waits:
            ops[i].inc = True
        counters = {}
        for op in ops:
            if op.dma is not None:
                op.semval = pos[op.idx]
            elif op.inc:
                key = ("eng", op.eng)
                counters[key] = counters.get(key, 0) + 1
                op.semval = (key, counters[key])
        waits = [[ops[d].semval for d in wl] for wl in wdeps]
        with contextlib.ExitStack() as st:
            sems = {}
            for e in ("pe", "act", "dve", "pool"):
                sems[("eng", e)] = st.enter_context(nc.semaphore("s_" + e))
            for g in dma_groups:
                sems[("dma", g)] = st.enter_context(nc.semaphore("d_" + g))
            block = st.enter_context(nc.Block())
            per_eng = {e: [op for op in ops if op.eng == e] for e in self.ENGS}

            def run(engobj, elist, ename):
                waited = {}
                for op in elist:
                    todo = []
                    for key, val in waits[op.idx]:
                        if waited.get(key, 0) < val:
                            todo.append((key, val))
                            waited[key] = val
                    attach = None
                    if todo and op.dma is None and ename in ("act", "dve", "pool"):
                        attach = todo.pop()
                    for key, val in todo:
                        engobj.wait_ge(sems[key], val)
                    res = op.fn(engobj)
                    if attach is not None:
                        assert not isinstance(res, (list, tuple))
                        res._wait_ge(sems[attach[0]], attach[1])
                    if op.dma is not None:
                        if not isinstance(res, (list, tuple)):
                            res = [res]
                        assert len(res) == op.ndma, (len(res), op.ndma)
                        for r in res:
                            r.then_inc(sems[op.semval[0]], 16)
                    elif op.inc:
                        if isinstance(res, (list, tuple)):
                            res = res[-1]
                        res.then_inc(sems[op.semval[0]], 1)
                if ename == "sp":
                    need = {}
                    for i in final_waits:
                        key, val = ops[i].semval
                        need[key] = max(need.get(key, 0), val)
                    for key, val in need.items():
                        if waited.get(key, 0) < val:
                            engobj.wait_ge(sems[key], val)
                            waited[key] = val

            @block.tensor
            def _(e):
                run(e, per_eng["pe"], "pe")

            @block.scalar
            def _(e):
                run(e, per_eng["act"], "act")

            @block.vector
            def _(e):
                run(e, per_eng["dve"], "dve")

            @block.gpsimd
            def _(e):
                run(e, per_eng["pool"], "pool")

            @block.sync
            def _(e):
                run(e, per_eng["sp"], "sp")


PP_COLS = {}
_pp_off = 0


def _pp(name, n):
    global _pp_off
    PP_COLS[name] = (_pp_off, n)
    _pp_off += n


_pp("norms", 32)
_pp("mu", 15)
_pp("w0", 4)
_pp("a0", 4)
_pp("kk", 4)
_pp("ka", 4)
_pp("rk", 4)
_pp("psc", 4)
_pp("lnwp", 4)
_pp("lnbp", 4)
_pp("ident", 128)
_pp("blk", 128)
_pp("ones", 128)
_pp("sel", 2)
_pp("maskkr", 256)
_pp("maskl", 128)
_pp("fix", 64)
PP_KEEP = _pp_off
_pp("scanmask", 512)
_pp("wup", 512)
_pp("aup", 512)
_pp("gup", 512)
_pp("wpool", 512)
PP_N = _pp_off


def _pack_params(inp):
    pp = np.zeros((128, PP_N), np.float32)

    def put(name, arr):
        o, n = PP_COLS[name]
        arr = np.asarray(arr, np.float32)
        pp[:arr.shape[0], o:o + arr.shape[1]] = arr

    def pc(v, nchunk):
        return np.asarray(v, np.float32).reshape(nchunk, 128).T

    norms = np.stack([pc(inp["ffn1_norm"][0], 8), pc(inp["mix_norm"][0], 8),
                      pc(inp["ffn2_norm"][0], 8), pc(inp["final_norm"], 8)], axis=1)
    put("norms", norms.reshape(128, 32))
    mu = np.asarray(inp["mu_shift"][0], np.float32)
    mut = np.zeros((128, 15), np.float32)
    mut[:, 0:12] = mu[0:1536].reshape(12, 128).T
    mut[0:32, 12] = mu[1536:1568]
    mut[0:32, 13] = mu[1568:1600]
    mut[0:96, 14] = mu[1600:1696]
    put("mu", mut)
    put("w0", pc(inp["w0"][0], 4))
    put("a0", pc(inp["a0"][0], 4))
    put("kk", pc(inp["k_k"][0], 4))
    put("ka", pc(inp["k_a"][0], 4))
    put("rk", pc(np.asarray(inp["r_k"][0]).reshape(512), 4))
    put("psc", pc(inp["pool_scale"][0], 4))
    put("lnwp", pc(inp["ln_w"][0], 4))
    put("lnbp", pc(inp["ln_b"][0], 4))
    put("wup", inp["w_lora_up"][0])
    put("aup", inp["a_lora_up"][0])
    put("gup", inp["g_lora_up"][0])
    put("wpool", np.transpose(np.asarray(inp["w_pool"][0], np.float32), (1, 0, 2)).reshape(128, 512))
    put("ident", np.eye(128, dtype=np.float32))
    blk = np.zeros((128, 128), np.float32)
    blk[0:64, 0:64] = 1
    blk[64:, 64:] = 1
    put("blk", blk)
    put("ones", np.ones((128, 128), np.float32))
    sel = np.zeros((128, 2), np.float32)
    sel[0:64, 0] = 1
    sel[64:, 1] = 1
    put("sel", sel)
    su = np.triu(np.ones((64, 64), np.float32), 1)
    ui = np.triu(np.ones((64, 64), np.float32), 0)
    def bd(m):
        z = np.zeros((128, 128), np.float32)
        z[0:64, 0:64] = m
        z[64:, 64:] = m
        return z
    put("maskkr", np.concatenate([bd(su), bd(ui)], 1))
    put("maskl", bd(su.T))
    sm = np.ones((128, 512), np.float32)
    sm[:, 0::64] = 0
    put("scanmask", sm)
    fix = np.ones((128, 4, 16), np.float32)
    for g, win in enumerate((2, 4, 8, 16)):
        t = np.arange(16)
        fix[:, g, :] = win / np.minimum(t + 1, win)
    put("fix", fix.reshape(128, 64))
    return pp


def build(NT):
    nc = bass.Bass("TRN2", target_bir_lowering=False)
    T = NT * TT
    dt_in = lambda name, shape: nc.dram_tensor(name, shape, F32, kind="ExternalInput").ap()
    xT_d = dt_in("xTin", [D, T])
    pp_d = dt_in("ppin", [128, PP_N])
    wsrc = {
        "g1": dt_in("w_g1", [D, DFF]), "u1": dt_in("w_u1", [D, DFF]), "d1": dt_in("w_d1", [DFF, D]),
        "win": dt_in("w_win", [D, INC]), "wout": dt_in("w_wout", [D, D]),
        "g2": dt_in("w_g2", [D, DFF]), "u2": dt_in("w_u2", [D, DFF]), "d2": dt_in("w_d2", [DFF, D]),
    }
    out_d = nc.dram_tensor("outTd", [D, T], F32, kind="ExternalOutput").ap()
    wbf = {k: nc.dram_tensor("bf_" + k, list(v.shape), BF16, kind="Internal").ap() for k, v in wsrc.items()}

    S = Sched(nc)
    st = contextlib.ExitStack()
    with st:
        sb = lambda name, shape, dt: st.enter_context(nc.sbuf_tensor(name, shape, dt))
        PP = sb("PP", [128, PP_KEEP], F32)
        xT = sb("xT", [128, 8, TT], F32)
        xn = sb("xn", [128, 8, TT], BF16)
        RING_N = 3
        ring = [sb("ring%d" % i, [128, 8, 512], BF16) for i in range(RING_N)]
        WD = sb("WD", [128, NF, D], BF16)
        rstd = sb("rstd", [128, TT], F32)
        identb = sb("identb", [128, 128], BF16)
        blkb = sb("blkb", [128, 128], BF16)
        onesb = sb("onesb", [128, 128], BF16)
        scanb = sb("scanb", [128, 512], BF16)
        wupb = sb("wupb", [32, 512], BF16)
        aupb = sb("aupb", [32, 512], BF16)
        gupb = sb("gupb", [96, 512], BF16)
        wpoolb = sb("wpoolb", [128, 512], BF16)
        omm = sb("omm", [128, 15], F32)
        omka = sb("omka", [128, 4], F32)
        carry = sb("carry", [128, 15], F32)
        halo = sb("halo", [128, 4, 16], F32)
        Hbd = sb("Hbd", [128, 4, 64], F32)
        Hbf = sb("Hbf", [128, 4, 64], BF16)
        Wc = sb("Wc", [128, 4, NCH], F32)
        small = sb("small", [128, 32], F32)
        dummy = sb("dummyt", [128, 1], F32)
        ARENA_BYTES = 102 * 1024
        arena = sb("arena", [128, ARENA_BYTES // 4], F32)
        arena_bf = arena.bitcast(BF16)

        def av(off_bytes, shape, dt):
            n = int(np.prod(shape[1:]))
            if dt == F32:
                assert off_bytes % 4 == 0
                base = arena[0:shape[0], off_bytes // 4: off_bytes // 4 + n]
            else:
                assert off_bytes % 2 == 0
                base = arena_bf[0:shape[0], off_bytes // 2: off_bytes // 2 + n]
            if len(shape) == 3:
                base = base.rearrange("p (a b) -> p a b", b=shape[2])
            return base

        KB = 1024
        Hff = av(0, [128, NF, TT], BF16)
        outT = av(22 * KB, [128, 8, TT], F32)
        SQ = av(88 * KB, [128, 8, TT], BF16)
        SG = [av(96 * KB + i * KB, [128, TT], BF16) for i in range(2)]
        Rf = av(0, [128, 4, TT], F32)
        YM = av(0, [128, 8, TT], BF16)
        PKf = av(8 * KB, [128, 4, TT], F32)
        VFb = av(16 * KB, [128, 4, TT], BF16)
        POOLP = av(20 * KB, [128, 4, 528], F32)
        TANHPW = av(28 * KB + 512, [32, TT], BF16)
        PAb = av(29 * KB + 512, [32, TT], BF16)
        SIGPG = av(30 * KB + 512, [96, TT], BF16)
        VT2 = av(32 * KB, [128, 4 * NCH, 64], BF16)
        KBTbd = av(36 * KB, [128, 4 * NCH, 128], BF16)
        BBTbd = av(44 * KB, [128, 4 * NCH, 128], BF16)
        QRbd = av(52 * KB, [128, 4 * NCH, 256], BF16)
        KTbd = av(68 * KB, [128, 4 * NCH, 128], BF16)
        BTbd = av(76 * KB, [128, 4 * NCH, 128], BF16)
        BONF = av(84 * KB, [128, 4, TT], BF16)
        tF = [av(88 * KB + i * 2 * KB, [128, TT], F32) for i in range(7)]
        praw = av(88 * KB, [128, 520], F32)
        tl = av(92 * KB, [128, TT], F32)
        KBbd = av(88 * KB, [128, NCH, 128], BF16)
        BBbd = av(90 * KB, [128, NCH, 128], BF16)
        Vbd = av(94 * KB, [128, NCH, 128], BF16)
        SQK = av(100 * KB, [128, TT], BF16)
        RKt = av(101 * KB, [128, TT], BF16)
        PBtmp = av(100 * KB, [128, TT], BF16)
        AKm = av(8 * KB, [128, 4, 256], BF16)
        ABm = av(10 * KB, [128, 4, 256], BF16)
        Mmb = [av(12 * KB + i * KB, [128, 4, 128], BF16) for i in range(2)]
        NTb = [av(14 * KB + i * 2 * KB, [128, 4, 256], BF16) for i in range(2)]
        Hdec = av(88 * KB, [128, 4, 64], F32)
        Ysb = av(89 * KB, [128, 4, 64], F32)
        tmpA = av(90 * KB, [128, 4, 64], F32)
        tmpB = av(91 * KB, [128, 4, 64], F32)
        ynbd = av(92 * KB, [128, 4, 128], BF16)
        Xb = av(93 * KB, [128, 4, 64], BF16)
        NUb = av(93 * KB + 512, [128, 4, 64], BF16)

        PS = st.enter_context(nc.psum_tensor("PS", [128, 8 * 512], F32))
        PSbf = PS.bitcast(BF16)
        bankB = [Buf("bank%d" % i, psum=True) for i in range(8)]

        def bank(i):
            return PS[:, i * 512:(i + 1) * 512]

        def bankbf(i):
            return PSbf[:, i * 1024:(i + 1) * 1024]

        B = {}

        def bb(name):
            if name not in B:
                B[name] = Buf(name)
            return B[name]

        ARENA = bb("ARENA")

        def barrier():
            S.add("pool", lambda e: e.memset(dummy[:], 0.0), writes=[ARENA, bb("dummy")])

        def ppv(name, rows=128):
            o, n = PP_COLS[name]
            return PP[0:rows, o:o + n]

        S.add("sp", lambda e: e.dma_start(out=PP[:], in_=pp_d[:, 0:PP_KEEP]), writes=[bb("PP")], dma="pp")
        STG = arena[:, 0:PP_N - PP_KEEP]
        S.add("sp", lambda e: e.dma_start(out=STG, in_=pp_d[:, PP_KEEP:PP_N]), writes=[bb("STG")], dma="pp2")

        def stg(name, rows=128):
            o, n = PP_COLS[name]
            return STG[0:rows, o - PP_KEEP:o - PP_KEEP + n]
        cvt_groups = {"c1": ["g1", "u1", "d1"], "c2": ["win", "wout"], "c3": ["g2", "u2", "d2"]}
        cvtB = {}
        for grp, names in cvt_groups.items():
            fns = []
            for nm in names:
                rows = wsrc[nm].shape[0]
                for r0 in range(0, rows, 128):
                    fns.append((nm, r0))

            def f(e, fns=fns):
                return [e.dma_start(out=wbf[nm][r0:r0 + 128, :], in_=wsrc[nm][r0:r0 + 128, :]) for nm, r0 in fns]
            for nm in names:
                cvtB[nm] = bb("cvt_" + grp)
            S.add("pool", f, writes=[bb("cvt_" + grp)], dma="cvt_" + grp, ndma=len(fns))

        cp = lambda eng, o, i, reads, writes: S.add(eng, lambda e: e.tensor_copy(o, i), reads=reads, writes=writes)
        cp("dve", identb[:], ppv("ident"), [bb("PP")], [bb("consts")])
        cp("dve", blkb[:], ppv("blk"), [bb("PP")], [bb("consts")])
        cp("dve", onesb[:], ppv("ones"), [bb("PP")], [bb("consts")])
        cp("dve", scanb[:], stg("scanmask"), [bb("STG")], [bb("consts")])
        cp("dve", wupb[:], stg("wup", 32), [bb("STG")], [bb("consts")])
        cp("dve", aupb[:], stg("aup", 32), [bb("STG")], [bb("consts")])
        cp("dve", gupb[:], stg("gup", 96), [bb("STG")], [bb("consts")])
        cp("dve", wpoolb[:], stg("wpool"), [bb("STG")], [bb("consts")])
        S.add("pool", lambda e: e.memset(dummy[:], 0.0), reads=[bb("consts")], writes=[ARENA, bb("dummy")])
        S.add("dve", lambda e: e.tensor_scalar(out=omm[:], in0=ppv("mu"), scalar1=-1.0, scalar2=1.0, op0=ALU.mult, op1=ALU.add),
              reads=[bb("PP")], writes=[bb("consts")])
        S.add("dve", lambda e: e.tensor_scalar(out=omka[:], in0=ppv("ka"), scalar1=-1.0, scalar2=1.0, op0=ALU.mult, op1=ALU.add),
              reads=[bb("PP")], writes=[bb("consts")])
        S.add("pool", lambda e: e.memset(carry[:], 0.0), writes=[bb("carry")])
        S.add("pool", lambda e: e.memset(halo[:], 0.0), writes=[bb("halo")])
        S.add("pool", lambda e: e.memset(Hbd[:], 0.0), writes=[bb("Hbd")])
        S.add("pool", lambda e: e.memset(Hbf[:], 0.0), writes=[bb("Hbf")])
        CONST = bb("consts")
        PPB = bb("PP")

        def unit_specs_ffn(gk, uk):
            sp = []
            for j in range(6):
                c0 = j * 512
                w = min(512, DFF - c0)
                sp.append((gk, c0, w))
                sp.append((uk, c0, w))
            return sp

        PH = os.environ.get("KPH", "f1,mix,f2").split(",")
        tile_specs = ((unit_specs_ffn("g1", "u1") if "f1" in PH else [])
                      + ([("win", 0, 512), ("win", 512, 512), ("win", 1024, 512), ("win", 1536, 160), ("win", 1696, 512),
                         ("wout", 0, 512), ("wout", 512, 512)] if "mix" in PH else [])
                      + (unit_specs_ffn("g2", "u2") if "f2" in PH else []))
        all_specs = tile_specs * NT
        ringB = [Buf("ring%d" % i) for i in range(RING_N)]
        wstate = {"issued": 0, "next": 0}

        def issue_load(n):
            nm, c0, w = all_specs[n]
            s = n % RING_N
            src = wbf[nm].rearrange("(c p) n -> p c n", p=128)[:, :, c0:c0 + w]
            S.add("sp", lambda e, s=s, src=src, w=w: e.dma_start(out=ring[s][:, :, 0:w], in_=src),
                  reads=[cvtB[nm]], writes=[ringB[s]], dma="ring%d" % s)

        def wget(expect):
            n = wstate["next"]
            assert all_specs[n][0] == expect, (all_specs[n], expect)
            while wstate["issued"] < min(len(all_specs), n + RING_N - 1):
                issue_load(wstate["issued"])
                wstate["issued"] += 1
            wstate["next"] = n + 1
            s = n % RING_N
            return ring[s], ringB[s], all_specs[n][2]

        WDB = bb("WD")

        def load_wd(nm):
            src = wbf[nm].rearrange("(f p) d -> p f d", p=128)

            def f(e):
                return [e.dma_start(out=WD[:, 0:11, :], in_=src[:, 0:11, :]),
                        e.dma_start(out=WD[:, 11:22, :], in_=src[:, 11:22, :])]
            S.add("sp", f, reads=[cvtB[nm]], writes=[WDB], dma="wd", ndma=2)

        XT = bb("xT")
        XN = bb("xn")
        RSTD = bb("rstd")

        def mm_group(out_ap, pairs, reads, obuf):
            n = len(pairs)

            def f(e):
                r = None
                for i, (l, rr) in enumerate(pairs):
                    r = e.matmul(out_ap, l, rr, start=(i == 0), stop=(i == n - 1))
                return r
            S.add("pe", f, reads=reads, writes=[obuf])

        XP = av(38 * KB, [128, 8, TT], F32)
        XPB = bb("XP")

        def rmsnorm_to_xn(ni, src=None, srcB=None, split=False):
            if src is None:
                src, srcB = xT, XT
            sqB = bb("SQ")
            for c in range(8):
                eng = "act" if c % 2 == 0 else "pool"
                if eng == "act":
                    S.add("act", lambda e, c=c: e.activation(SQ[:, c, :], src[:, c, :], AF.Square), reads=[srcB, ARENA], writes=[sqB])
                else:
                    S.add("pool", lambda e, c=c: e.tensor_tensor(out=SQ[:, c, :], in0=src[:, c, :], in1=src[:, c, :], op=ALU.mult),
                          reads=[srcB, ARENA], writes=[sqB])

            def rest():
                mm_group(bank(6), [(onesb[:], SQ[:, c, :]) for c in range(8)], [CONST, sqB, ARENA], bankB[6])
                S.add("act", lambda e: e.activation(rstd[:], bank(6), AF.Sqrt, scale=1.0 / D, bias=RMS_EPS), reads=[bankB[6]], writes=[RSTD])
                S.add("dve", lambda e: e.reciprocal(rstd[:], rstd[:]), reads=[RSTD], writes=[RSTD])
                o, _ = PP_COLS["norms"]
                for c in range(8):
                    gcol = PP[:, o + ni * 8 + c: o + ni * 8 + c + 1]
                    S.add("dve", lambda e, c=c, gcol=gcol: e.scalar_tensor_tensor(out=xn[:, c, :], in0=src[:, c, :], scalar=gcol, in1=rstd[:],
                                                                              op0=ALU.mult, op1=ALU.mult),
                          reads=[srcB, RSTD, PPB, ARENA], writes=[XN])
            if split:
                return rest
            rest()

        def ffn(gk, uk, dk, ni, skip_norm=False, mid_hook=None):
            HB = bb("Hff")
            if not skip_norm:
                rmsnorm_to_xn(ni)
            f = 0
            for j in range(6):
                gt, gB, w = wget(gk)
                ut, uB, _ = wget(uk)
                for jj in range(w // 128):
                    gb = f % 2
                    ub = 2 + f % 2
                    mm_group(bank(gb), [(gt[:, c, jj * 128:(jj + 1) * 128], xn[:, c, :]) for c in range(8)], [gB, XN], bankB[gb])
                    mm_group(bank(ub), [(ut[:, c, jj * 128:(jj + 1) * 128], xn[:, c, :]) for c in range(8)], [uB, XN], bankB[ub])
                    sgB = bb("SG%d" % (f % 2))
                    S.add("act", lambda e, gb=gb, f=f: e.activation(SG[f % 2][:], bank(gb), AF.Silu), reads=[bankB[gb], ARENA], writes=[sgB])
                    S.add("dve", lambda e, ub=ub, f=f: e.tensor_tensor(out=Hff[:, f, :], in0=bank(ub), in1=SG[f % 2][:], op=ALU.mult),
                          reads=[bankB[ub], sgB, ARENA], writes=[HB])
                    f += 1
            late = mid_hook() if mid_hook is not None else None
            for d in range(8):
                ob = 4 + d % 2
                mm_group(bank(ob), [(WD[:, ff, d * 128:(d + 1) * 128], Hff[:, ff, :]) for ff in range(NF)], [WDB, HB, ARENA], bankB[ob])
                S.add("dve", lambda e, d=d, ob=ob: e.scalar_tensor_tensor(out=xT[:, d, :], in0=bank(ob), scalar=0.5, in1=xT[:, d, :],
                                                                      op0=ALU.mult, op1=ALU.add),
                      reads=[bankB[ob], XT], writes=[XT])
                if d == 3 and late is not None:
                    late()

        mu_o = PP_COLS["mu"][0]

        def lerp_evict(psb, rows, idx, dest, destB, dest_reads=()):
            prB = bb("praw")
            muc = PP[0:rows, mu_o + idx:mu_o + idx + 1]
            S.add("act", lambda e: e.activation(praw[0:rows, 1:513], bank(psb)[0:rows, :], AF.Copy, scale=muc), reads=[bankB[psb], PPB, ARENA], writes=[prB])
            S.add("act", lambda e: e.activation(tl[0:rows, :], bank(psb)[0:rows, :], AF.Copy, scale=omm[0:rows, idx:idx + 1]), reads=[bankB[psb], CONST, ARENA], writes=[bb("tl")])
            S.add("pool", lambda e: e.tensor_copy(praw[0:rows, 0:1], carry[0:rows, idx:idx + 1]), reads=[bb("carry"), ARENA], writes=[prB])
            S.add("dve", lambda e: e.tensor_tensor(out=dest, in0=praw[0:rows, 0:512], in1=tl[0:rows, :], op=ALU.add),
                  reads=[prB, bb("tl"), ARENA] + list(dest_reads), writes=[destB])
            S.add("pool", lambda e: e.tensor_copy(carry[0:rows, idx:idx + 1], praw[0:rows, 512:513]), reads=[prB, ARENA], writes=[bb("carry")])

        KCUT = int(os.environ.get("KCUT", "9"))
        KC2 = int(os.environ.get("KC2", "9"))

        def mixer_tail():
            wget("wout")
            wget("wout")
            barrier()

        def mixer(ti):
            rmsnorm_to_xn(1)
            barrier()
            RB, PKB, VFB = bb("Rf"), bb("PKf"), bb("VFb")
            pbank = [0]

            def nb():
                pbank[0] = (pbank[0] + 1) % 4
                return pbank[0]
            for qi, (dest3, dB) in enumerate(((Rf, RB), (PKf, PKB), (VFb, VFB))):
                wt, wB, _ = wget("win")
                for g in range(4):
                    b_ = nb()
                    mm_group(bank(b_), [(wt[:, c, g * 128:(g + 1) * 128], xn[:, c, :]) for c in range(8)], [wB, XN], bankB[b_])
                    lerp_evict(b_, 128, qi * 4 + g, dest3[:, g, :], dB)
            wt, wB, _ = wget("win")
            LB = bb("lora")
            for li, (c0, rows, idx, dst, func) in enumerate(((0, 32, 12, TANHPW, AF.Tanh), (32, 32, 13, PAb, AF.Copy), (64, 96, 14, SIGPG, AF.Sigmoid))):
                b_ = nb()
                mm_group(bank(b_)[0:rows, :], [(wt[:, c, c0:c0 + rows], xn[:, c, :]) for c in range(8)], [wB, XN], bankB[b_])
                lt = tF[3]
                lerp_evict(b_, rows, idx, lt[0:rows, :], bb("tF3"))
                S.add("act", lambda e, dst=dst, rows=rows, func=func, lt=lt: e.activation(dst[0:rows, :], lt[0:rows, :], func),
                      reads=[bb("tF3"), ARENA], writes=[LB])
            wt, wB, _ = wget("win")
            PPOOL = bb("POOLP")
            for g in range(4):
                b_ = nb()
                mm_group(bank(b_), [(wt[:, c, g * 128:(g + 1) * 128], xn[:, c, :]) for c in range(8)], [wB, XN], bankB[b_])
                S.add("act", lambda e, g=g, b_=b_: e.activation(POOLP[:, g, 16:528], bank(b_), AF.Copy), reads=[bankB[b_], ARENA], writes=[PPOOL])
            S.add("pool", lambda e: e.tensor_copy(POOLP[:, :, 0:16], halo[:]), reads=[bb("halo"), ARENA], writes=[PPOOL])

            barrier()
            if KCUT <= 1:
                return mixer_tail()
            tB = [bb("tF%d" % i) for i in range(7)]
            CH = bb("chunkops")
            v3 = lambda ap: ap.rearrange("p (c t) -> p c t", t=C)

            sel_o = PP_COLS["sel"][0]

            def bdmul(dst, blk0, col0, in0, in1, reads, writes):
                for par in range(2):
                    mcol = PP[:, sel_o + par: sel_o + par + 1]
                    o = dst[:, blk0:blk0 + NCH, col0 + par * 64: col0 + par * 64 + 64]
                    if in1 is None:
                        S.add("act", lambda e, o=o, mcol=mcol: e.activation(o, v3(in0), AF.Copy, scale=mcol),
                              reads=list(reads) + [PPB], writes=writes)
                    else:
                        S.add("dve", lambda e, o=o, mcol=mcol: e.scalar_tensor_tensor(out=o, in0=v3(in0), scalar=mcol, in1=v3(in1), op0=ALU.mult, op1=ALU.mult),
                              reads=list(reads) + [PPB], writes=writes)

            for g in range(4):
                lam, a_, L_, kk_, nrm, k5, b6 = tF
                lamB, aB, LB_, kkB, nrmB, k5B, b6B = tB
                E_, EB = nrm, nrmB
                w0c = PP[:, PP_COLS["w0"][0] + g: PP_COLS["w0"][0] + g + 1]
                a0c = PP[:, PP_COLS["a0"][0] + g: PP_COLS["a0"][0] + g + 1]
                kkc = PP[:, PP_COLS["kk"][0] + g: PP_COLS["kk"][0] + g + 1]
                kac = PP[:, PP_COLS["ka"][0] + g: PP_COLS["ka"][0] + g + 1]
                rkc = PP[:, PP_COLS["rk"][0] + g: PP_COLS["rk"][0] + g + 1]
                gs = slice(g * 128, (g + 1) * 128)
                g8 = g * NCH
                mm_group(bank(4), [(wupb[:, gs], TANHPW[:, :])], [CONST, LB, ARENA], bankB[4])
                S.add("act", lambda e, w0c=w0c: e.activation(lam[:], bank(4), AF.Sigmoid, bias=w0c), reads=[bankB[4], PPB, ARENA], writes=[lamB])
                mm_group(bank(5), [(aupb[:, gs], PAb[:, :])], [CONST, LB, ARENA], bankB[5])
                S.add("act", lambda e, a0c=a0c: e.activation(a_[:], bank(5), AF.Sigmoid, bias=a0c), reads=[bankB[5], PPB, ARENA], writes=[aB])
                S.add("dve", lambda e: e.tensor_tensor_scan(out=L_[:], data0=scanb[:], data1=lam[:], initial=0.0, op0=ALU.mult, op1=ALU.add),
                      reads=[lamB, CONST, ARENA], writes=[LB_])
                S.add("act", lambda e, g=g, kkc=kkc: e.activation(kk_[:], PKf[:, g, :], AF.Copy, scale=kkc),
                      reads=[PKB, PPB, ARENA], writes=[kkB])
                S.add("act", lambda e: e.activation(SQK[:], kk_[:], AF.Square), reads=[kkB, ARENA], writes=[b6B])
                mm_group(bank(6), [(blkb[:], SQK[:])], [CONST, b6B, ARENA], bankB[6])
                S.add("act", lambda e: e.activation(nrm[:], bank(6), AF.Sqrt), reads=[bankB[6], ARENA], writes=[nrmB])
                S.add("dve", lambda e: e.tensor_scalar(out=nrm[:], in0=nrm[:], scalar1=1e-12, scalar2=None, op0=ALU.max), reads=[nrmB, ARENA], writes=[nrmB])
                S.add("dve", lambda e: e.reciprocal(nrm[:], nrm[:]), reads=[nrmB, ARENA], writes=[nrmB])
                S.add("dve", lambda e: e.tensor_tensor(out=kk_[:], in0=kk_[:], in1=nrm[:], op=ALU.mult), reads=[kkB, nrmB, ARENA], writes=[kkB])
                S.add("act", lambda e, g=g, kac=kac: e.activation(k5[:], a_[:], AF.Identity, scale=kac, bias=omka[:, g:g + 1]),
                      reads=[aB, PPB, CONST, ARENA], writes=[k5B])
                S.add("dve", lambda e, g=g: e.tensor_tensor(out=k5[:], in0=k5[:], in1=PKf[:, g, :], op=ALU.mult), reads=[k5B, PKB, ARENA], writes=[k5B])
                S.add("dve", lambda e, g=g, rkc=rkc: e.scalar_tensor_tensor(out=RKt[:], in0=Rf[:, g, :], scalar=rkc, in1=k5[:], op0=ALU.mult, op1=ALU.mult),
                      reads=[RB, k5B, PPB, ARENA], writes=[b6B])
                mm_group(bank(5), [(blkb[:], RKt[:])], [CONST, b6B, ARENA], bankB[5])
                S.add("dve", lambda e, g=g: e.tensor_tensor(out=BONF[:, g, :], in0=bank(5), in1=VFb[:, g, :], op=ALU.mult), reads=[bankB[5], VFB, ARENA], writes=[CH])
                S.add("dve", lambda e: e.tensor_tensor(out=b6[:], in0=kk_[:], in1=a_[:], op=ALU.mult), reads=[kkB, aB, ARENA], writes=[b6B])
                S.add("act", lambda e: e.activation(E_[:], L_[:], AF.Exp, scale=-C0), reads=[LB_, ARENA], writes=[EB])
                bdmul(QRbd, g8, 128, Rf[:, g, :], E_[:], [RB, EB, ARENA], [CH])
                S.add("act", lambda e: e.activation(E_[:], L_[:], AF.Exp, scale=C0), reads=[LB_, ARENA], writes=[EB])
                bdmul(KTbd, g8, 0, k5[:], E_[:], [k5B, EB, ARENA], [CH])
                bdmul(BTbd, g8, 0, b6[:], E_[:], [b6B, EB, ARENA], [CH])
                Lend = L_[:].rearrange("p (c t) -> p c t", t=C)[:, :, C - 1:C]
                S.add("act", lambda e, g=g, Lend=Lend: e.activation(Wc[:, g, :].unsqueeze(2), Lend, AF.Exp, scale=-C0), reads=[LB_, ARENA], writes=[CH])
                S.add("dve", lambda e: e.tensor_tensor(out=lam[:], in0=L_[:], in1=lam[:], op=ALU.subtract), reads=[LB_, lamB, ARENA], writes=[lamB])
                S.add("act", lambda e: e.activation(E_[:], lam[:], AF.Exp, scale=-C0), reads=[lamB, ARENA], writes=[EB])
                bdmul(QRbd, g8, 0, kk_[:], E_[:], [kkB, EB, ARENA], [CH])
                S.add("dve", lambda e, Lend=Lend: e.tensor_tensor(out=lam[:].rearrange("p (c t) -> p c t", t=C), in0=L_[:].rearrange("p (c t) -> p c t", t=C),
                                                                 in1=Lend.broadcast_to([128, NCH, C]), op=ALU.subtract),
                      reads=[LB_, lamB, ARENA], writes=[lamB])
                S.add("act", lambda e: e.activation(E_[:], lam[:], AF.Exp, scale=C0), reads=[lamB, ARENA], writes=[EB])
                bdmul(KBbd, 0, 0, k5[:], E_[:], [k5B, EB, ARENA], [lamB])
                bdmul(BBbd, 0, 0, b6[:], E_[:], [b6B, EB, ARENA], [aB])
                bdmul(Vbd, 0, 0, VFb[:, g, :], None, [VFB, ARENA], [kkB])
                for qi, (src, srcB) in enumerate(((Vbd, kkB), (KBbd, lamB), (BBbd, aB))):
                    tbk = 7 if qi % 2 == 0 else 3
                    tp = bankbf(tbk).rearrange("p (c n) -> p c n", n=128)

                    def ftr(e, src=src, tp=tp):
                        r = None
                        for c in range(NCH):
                            r = e.transpose(tp[:, c, :], src[:, c, :], identb[:])
                        return r
                    S.add("pe", ftr, reads=[srcB, CONST, ARENA], writes=[bankB[tbk]])
                    if qi == 0:
                        S.add("act", lambda e, tp=tp, g8=g8: e.activation(VT2[0:64, g8:g8 + NCH, :], tp[0:64, :, 0:64], AF.Copy), reads=[bankB[tbk], ARENA], writes=[CH])
                        S.add("dve", lambda e, tp=tp, g8=g8: e.tensor_copy(VT2[64:128, g8:g8 + NCH, :], tp[64:128, :, 64:128]), reads=[bankB[tbk], ARENA], writes=[CH])
                    elif qi == 1:
                        S.add("act", lambda e, tp=tp, g8=g8: e.activation(KBTbd[:, g8:g8 + NCH, :], tp, AF.Copy), reads=[bankB[tbk], ARENA], writes=[CH])
                    else:
                        S.add("dve", lambda e, tp=tp, g8=g8: e.tensor_copy(BBTbd[:, g8:g8 + NCH, :], tp), reads=[bankB[tbk], ARENA], writes=[CH])
            barrier()

            if KCUT <= 2:
                return mixer_tail()
            YMB = bb("YM")
            tPQ = av(96 * KB, [128, 528], F32)
            tQQ = av(8 * KB + 0, [128, 528], F32)
            PQB, QQB = bb("tPQ"), bb("tQQ")
            for g, win in enumerate((2, 4, 8, 16)):
                B0 = POOLP[:, g, :]
                bufs = [(tPQ, PQB), (tQQ, QQB)]
                cur, curB = B0, PPOOL
                sh = 1
                k_ = 0
                while sh < win:
                    dst, dstB = bufs[k_ % 2]
                    lo = 2 * sh - 1
                    S.add("pool", lambda e, dst=dst, cur=cur, lo=lo, sh=sh: e.tensor_tensor(out=dst[:, lo:528], in0=cur[:, lo:528], in1=cur[:, lo - sh:528 - sh], op=ALU.add),
                          reads=[curB, ARENA], writes=[dstB])
                    cur, curB = dst, dstB
                    sh *= 2
                    k_ += 1
                if ti == 0:
                    fo = PP_COLS["fix"][0]
                    S.add("dve", lambda e, cur=cur, g=g, fo=fo: e.tensor_tensor(out=cur[:, 16:32], in0=cur[:, 16:32], in1=PP[:, fo + g * 16: fo + (g + 1) * 16], op=ALU.mult),
                          reads=[curB, PPB, ARENA], writes=[curB])
                S.add("dve", lambda e, cur=cur, B0=B0, win=win: e.scalar_tensor_tensor(out=PBtmp[:], in0=cur[:, 16:528], scalar=1.0 / win, in1=B0[:, 16:528], op0=ALU.mult, op1=ALU.subtract),
                      reads=[curB, PPOOL, ARENA], writes=[bb("PBtmp")])
                mm_group(bank(4 + g % 2), [(wpoolb[:, g * 128:(g + 1) * 128], PBtmp[:])], [CONST, bb("PBtmp"), ARENA], bankB[4 + g % 2])
                psc = PP[:, PP_COLS["psc"][0] + g: PP_COLS["psc"][0] + g + 1]
                S.add("act", lambda e, g=g, psc=psc: e.activation(YM[:, 4 + g, :], bank(4 + g % 2), AF.Identity, scale=psc), reads=[bankB[4 + g % 2], PPB, ARENA], writes=[YMB])
                S.add("pool", lambda e, g=g: e.tensor_copy(halo[:, g, :], POOLP[:, g, 512:528]), reads=[PPOOL, ARENA], writes=[bb("halo")])

            barrier()
            if KCUT <= 3:
                return mixer_tail()
            HB_, HFB = bb("Hbd"), bb("Hbf")
            mkr_b = ppv("maskkr").unsqueeze(1).broadcast_to([128, 4, 256])
            mlo_b = ppv("maskl").unsqueeze(1).broadcast_to([128, 4, 128])
            idf_b = ppv("ident").unsqueeze(1).broadcast_to([128, 4, 128])
            AKs = [AKm, av(20 * KB, [128, 4, 256], BF16)]
            ABs = [ABm, av(22 * KB, [128, 4, 256], BF16)]
            NTs = [NTb, [av(24 * KB + i * 2 * KB, [128, 4, 256], BF16) for i in range(2)]]
            Mms = [Mmb, [av(94 * KB + i * KB, [128, 4, 128], BF16) for i in range(2)]]
            AKBs = [bb("AKm0"), bb("AKm1")]
            ABBs = [bb("ABm0"), bb("ABm1")]
            MBs = [[bb("Mm00"), bb("Mm01")], [bb("Mm10"), bb("Mm11")]]
            NBs = [[bb("NT00"), bb("NT01")], [bb("NT10"), bb("NT11")]]
            XB_, NUB = bb("Xb"), bb("NUb")
            YNB = bb("ynbd")
            S.add("pool", lambda e: e.memset(ynbd[:], 0.0), reads=[ARENA], writes=[YNB])
            psK = PS[:, 0:1024].rearrange("p (g n) -> p g n", n=256)
            psB = PS[:, 1024:2048].rearrange("p (g n) -> p g n", n=256)
            psM = bank(4).rearrange("p (g n) -> p g n", n=128)
            ps1 = PS[:, 0:1024].rearrange("p (g n) -> p g n", n=256)
            ps2 = bank(2).rearrange("p (g n) -> p g n", n=128)
            psX = bank(5)[:, 0:256].rearrange("p (g n) -> p g n", n=64)
            psU = bank(5)[:, 256:512].rearrange("p (g n) -> p g n", n=64)
            psH = bank(6)[:, 0:256].rearrange("p (g n) -> p g n", n=64)
            psY = bank(6)[:, 256:512].rearrange("p (g n) -> p g n", n=64)
            ytp = bankbf(7)[:, 0:512].rearrange("p (g n) -> p g n", n=128)

            def stageA(c):
                p = c % 2
                AK, AB, NT_, Mm_ = AKs[p], ABs[p], NTs[p], Mms[p]
                AKB, ABB, NB, MB = AKBs[p], ABBs[p], NBs[p], MBs[p]

                def fA(e):
                    r = None
                    for g in range(4):
                        i = g * NCH + c
                        e.matmul(psK[:, g, :], KTbd[:, i, :], QRbd[:, i, :], start=True, stop=True)
                        e.matmul(psB[:, g, :], BTbd[:, i, :], QRbd[:, i, :], start=True, stop=True)
                        r = e.matmul(psM[:, g, :], QRbd[:, i, 0:128], BTbd[:, i, :], start=True, stop=True)
                    return r
                S.add("pe", fA, reads=[CH, ARENA], writes=[bankB[0], bankB[1], bankB[2], bankB[3], bankB[4]])
                S.add("dve", lambda e: e.tensor_tensor(out=AK[:], in0=psK, in1=mkr_b, op=ALU.mult),
                      reads=[bankB[0], bankB[1], PPB, ARENA], writes=[AKB])
                S.add("dve", lambda e: e.tensor_tensor(out=AB[:], in0=psB, in1=mkr_b, op=ALU.mult),
                      reads=[bankB[2], bankB[3], PPB, ARENA], writes=[ABB])
                S.add("dve", lambda e: e.tensor_tensor(out=Mm_[0][:], in0=psM, in1=mlo_b, op=ALU.mult),
                      reads=[bankB[4], PPB, ARENA], writes=[MB[0]])
                S.add("dve", lambda e: e.tensor_tensor(out=NT_[1][:, :, 0:128], in0=idf_b, in1=AB[:, :, 0:128], op=ALU.subtract),
                      reads=[ABB, PPB, ARENA], writes=[NB[1]])
                yield

                def f0(e):
                    r = None
                    for g in range(4):
                        e.matmul(ps1[:, g, 128:256], Mm_[0][:, g, :], AB[:, g, 0:128], start=True, stop=True)
                        r = e.matmul(ps2[:, g, :], AB[:, g, 0:128], Mm_[0][:, g, :], start=True, stop=True)
                    return r
                S.add("pe", f0, reads=[MB[0], ABB, ARENA], writes=[bankB[0], bankB[1], bankB[2]])
                S.add("act", lambda e: e.activation(Mm_[1][:], ps2, AF.Copy), reads=[bankB[2], ARENA], writes=[MB[1]])
                S.add("dve", lambda e: e.tensor_copy(NT_[1][:, :, 128:256], ps1[:, :, 128:256]), reads=[bankB[0], bankB[1], ARENA], writes=[NB[1]])
                yield
                cur = 1
                for j in range(5):
                    nx = 1 - cur
                    last = (j == 4)

                    def fl(e, cur=cur, last=last):
                        r = None
                        for g in range(4):
                            if last:
                                r = e.matmul(ps1[:, g, 0:128], Mm_[cur][:, g, :], NT_[cur][:, g, 0:128], start=True, stop=True)
                            else:
                                e.matmul(ps1[:, g, :], Mm_[cur][:, g, :], NT_[cur][:, g, :], start=True, stop=True)
                                r = e.matmul(ps2[:, g, :], NT_[cur][:, g, 128:256], Mm_[cur][:, g, :], start=True, stop=True)
                        return r
                    S.add("pe", fl, reads=[MB[cur], NB[cur], ARENA], writes=[bankB[0], bankB[1]] + ([] if last else [bankB[2]]))
                    S.add("dve", lambda e, cur=cur, nx=nx: e.tensor_tensor(out=NT_[nx][:, :, 0:128], in0=ps1[:, :, 0:128], in1=NT_[cur][:, :, 0:128], op=ALU.add),
                          reads=[bankB[0], bankB[1], NB[cur], ARENA], writes=[NB[nx]])
                    if not last:
                        S.add("act", lambda e, nx=nx: e.activation(Mm_[nx][:], ps2, AF.Copy), reads=[bankB[2], ARENA], writes=[MB[nx]])
                        if j < 3:
                            S.add("dve", lambda e, nx=nx: e.tensor_copy(NT_[nx][:, :, 128:256], ps1[:, :, 128:256]), reads=[bankB[0], bankB[1], ARENA], writes=[NB[nx]])
                    cur = nx
                    yield
                assert cur == 0

            def stageB(c):
                p = c % 2
                cs = slice(c * C, (c + 1) * C)
                AK, AB = AKs[p], ABs[p]
                AKB, ABB = AKBs[p], ABBs[p]
                PT = NTs[p][0][:, :, 0:128]
                PTB = NBs[p][0]

                def fX(e):
                    r = None
                    for g in range(4):
                        i = g * NCH + c
                        e.matmul(psX[:, g, :], QRbd[:, i, 0:128], Hbf[:, g, :], start=True, stop=False)
                        r = e.matmul(psX[:, g, :], AK[:, g, 0:128], VT2[:, i, :], start=False, stop=True)
                    return r
                S.add("pe", fX, reads=[CH, HFB, AKB, ARENA], writes=[bankB[5]])
                S.add("dve", lambda e: e.tensor_copy(Xb[:], psX), reads=[bankB[5], ARENA], writes=[XB_])
                yield

                def fU(e):
                    r = None
                    for g in range(4):
                        r = e.matmul(psU[:, g, :], PT[:, g, :], Xb[:, g, :], start=True, stop=True)
                    return r
                S.add("pe", fU, reads=[PTB, XB_, ARENA], writes=[bankB[5]])
                S.add("act", lambda e: e.activation(NUb[:], psU, AF.Copy, scale=-1.0), reads=[bankB[5], ARENA], writes=[NUB])
                yield

                def fHY(e):
                    r = None
                    for g in range(4):
                        i = g * NCH + c
                        e.matmul(psH[:, g, :], KBTbd[:, i, :], VT2[:, i, :], start=True, stop=False, skip_group_check=True)
                        r = e.matmul(psH[:, g, :], BBTbd[:, i, :], NUb[:, g, :], start=False, stop=True, skip_group_check=True)
                    for g in range(4):
                        i = g * NCH + c
                        e.matmul(psY[:, g, :], QRbd[:, i, 128:256], Hbf[:, g, :], start=(g == 0), stop=False, skip_group_check=True)
                    for g in range(4):
                        i = g * NCH + c
                        e.matmul(psY[:, g, :], AK[:, g, 128:256], VT2[:, i, :], start=False, stop=False, skip_group_check=True)
                        r = e.matmul(psY[:, g, :], AB[:, g, 128:256], NUb[:, g, :], start=False, stop=True, skip_group_check=True)
                    return r
                S.add("pe", fHY, reads=[CH, HFB, AKB, ABB, NUB, ARENA], writes=[bankB[6]])
                YB, TA, TB_, SM = bb("Ysb"), bb("tmpA"), bb("tmpB"), bb("small")
                wcb = Wc[:, :, c:c + 1].broadcast_to([128, 4, 64])
                S.add("dve", lambda e, wcb=wcb: e.tensor_tensor(out=Hdec[:], in0=Hbd[:], in1=wcb, op=ALU.mult), reads=[HB_, CH, ARENA], writes=[bb("Hdec")])
                S.add("dve", lambda e: e.tensor_tensor(out=Hbd[:], in0=psH, in1=Hdec[:], op=ALU.add), reads=[bankB[6], bb("Hdec"), ARENA], writes=[HB_])
                S.add("act", lambda e: e.activation(Hbf[:], Hbd[:], AF.Copy), reads=[HB_], writes=[HFB])
                S.add("dve", lambda e: e.tensor_copy(Ysb[:], psY), reads=[bankB[6], ARENA], writes=[YB])
                yield
                s1, s2, mean, msq, var = small[:, 0:4], small[:, 4:8], small[:, 8:12], small[:, 12:16], small[:, 16:20]
                S.add("dve", lambda e: e.tensor_reduce(out=s1, in_=Ysb[:], axis=AX.X, op=ALU.add), reads=[YB, ARENA], writes=[SM])
                S.add("act", lambda e: e.activation(tmpA[:], Ysb[:], AF.Square), reads=[YB, ARENA], writes=[TA])
                S.add("dve", lambda e: e.tensor_reduce(out=s2, in_=tmpA[:], axis=AX.X, op=ALU.add), reads=[TA, SM, ARENA], writes=[SM])
                S.add("dve", lambda e: e.tensor_scalar(out=mean, in0=s1, scalar1=1.0 / 64, scalar2=None, op0=ALU.mult), reads=[SM], writes=[SM])
                S.add("dve", lambda e: e.tensor_tensor(out=msq, in0=mean, in1=mean, op=ALU.mult), reads=[SM], writes=[SM])
                S.add("dve", lambda e: e.scalar_tensor_tensor(out=var, in0=s2, scalar=1.0 / 64, in1=msq, op0=ALU.mult, op1=ALU.subtract), reads=[SM], writes=[SM])
                S.add("act", lambda e: e.activation(var, var, AF.Sqrt, bias=GN_EPS), reads=[SM], writes=[SM])
                S.add("dve", lambda e: e.reciprocal(var, var), reads=[SM], writes=[SM])
                mean_b = mean.unsqueeze(2).broadcast_to([128, 4, 64])
                S.add("dve", lambda e, mean_b=mean_b: e.tensor_tensor(out=tmpB[:], in0=Ysb[:], in1=mean_b, op=ALU.subtract), reads=[YB, SM, ARENA], writes=[TB_])
                for par in range(2):
                    ps_ = slice(par * 64, par * 64 + 64)
                    rb = var[ps_, :].unsqueeze(2).broadcast_to([64, 4, 64])
                    S.add("dve", lambda e, ps_=ps_, rb=rb, par=par: e.tensor_tensor(out=ynbd[ps_, :, par * 64:par * 64 + 64], in0=tmpB[ps_, :, :], in1=rb, op=ALU.mult),
                          reads=[TB_, SM, ARENA], writes=[YNB])

                def fyt(e):
                    r = None
                    for g in range(4):
                        r = e.transpose(ytp[:, g, :], ynbd[:, g, :], identb[:])
                    return r
                S.add("pe", fyt, reads=[YNB, CONST, ARENA], writes=[bankB[7]])
                S.add("act", lambda e: e.activation(YM[0:64, 0:4, cs], ytp[0:64, :, 0:64], AF.Copy), reads=[bankB[7], ARENA], writes=[YMB])
                S.add("dve", lambda e: e.tensor_copy(YM[64:128, 0:4, cs], ytp[64:128, :, 64:128]), reads=[bankB[7], ARENA], writes=[YMB])
                yield

            for _ in stageA(0):
                pass
            for c in range(NCH):
                gens = [stageB(c)] + ([stageA(c + 1)] if c + 1 < NCH else [])
                while gens:
                    for gn in list(gens):
                        try:
                            next(gn)
                        except StopIteration:
                            gens.remove(gn)

            tE, tEB = tF[4], bb("tF4")
            lw0, lb0 = PP_COLS["lnwp"][0], PP_COLS["lnbp"][0]
            for g in range(4):
                gs = slice(g * 128, (g + 1) * 128)
                gbk = 5 + g % 2
                S.add("act", lambda e, g=g: e.activation(tE[:], YM[:, g, :], AF.Identity, scale=PP[:, lw0 + g:lw0 + g + 1], bias=PP[:, lb0 + g:lb0 + g + 1]),
                      reads=[YMB, PPB, ARENA], writes=[tEB])
                S.add("dve", lambda e, g=g: e.tensor_tensor(out=tE[:], in0=tE[:], in1=BONF[:, g, :], op=ALU.add), reads=[tEB, CH, ARENA], writes=[tEB])
                mm_group(bank(gbk), [(gupb[:, gs], SIGPG[:, :])], [CONST, LB, ARENA], bankB[gbk])
                S.add("dve", lambda e, g=g, gbk=gbk: e.tensor_tensor(out=YM[:, g, :], in0=bank(gbk), in1=tE[:], op=ALU.mult),
                      reads=[bankB[gbk], tEB, ARENA], writes=[YMB])

            if KCUT <= 4:
                return mixer_tail()
            w0t, w0B, _ = wget("wout")
            w1t, w1B, _ = wget("wout")
            for d in range(8):
                wt, wB = (w0t, w0B) if d < 4 else (w1t, w1B)
                dd = d % 4
                ob = d % 2
                mm_group(bank(ob), [(wt[:, cc, dd * 128:(dd + 1) * 128], YM[:, cc, :]) for cc in range(8)], [wB, YMB, ARENA], bankB[ob])
                S.add("dve", lambda e, d=d, ob=ob: e.tensor_tensor(out=xT[:, d, :], in0=bank(ob), in1=xT[:, d, :], op=ALU.add),
                      reads=[bankB[ob], XT], writes=[XT])
            barrier()

        finals = []
        OUTB = bb("outT")
        for ti in range(NT):
            ts = slice(ti * TT, (ti + 1) * TT)
            xsrc = lambda t_: xT_d.rearrange("(c p) t -> p c t", p=128)[:, :, t_ * TT:(t_ + 1) * TT]
            FULL = ("f1" in PH and "mix" in PH and "f2" in PH)
            pre = FULL and ti > 0
            if not pre:
                src = xsrc(ti)
                S.add("sp", lambda e, src=src: [e.dma_start(out=xT[:, 0:4, :], in_=src[:, 0:4, :]), e.dma_start(out=xT[:, 4:8, :], in_=src[:, 4:8, :])],
                      writes=[XT], dma="xin", ndma=2)
            else:
                for c in range(8):
                    eng = ("dve", "pool", "act")[c % 3]
                    if eng == "act":
                        S.add("act", lambda e, c=c: e.activation(xT[:, c, :], XP[:, c, :], AF.Copy), reads=[XPB, ARENA], writes=[XT])
                    else:
                        S.add(eng, lambda e, c=c: e.tensor_copy(xT[:, c, :], XP[:, c, :]), reads=[XPB, ARENA], writes=[XT])
            if "f1" in PH:
                load_wd("d1")
                ffn("g1", "u1", "d1", 0, skip_norm=pre)
            if "mix" in PH:
                mixer(ti)
            hook = None
            if FULL and ti + 1 < NT:
                src = xsrc(ti + 1)
                S.add("sp", lambda e, src=src: [e.dma_start(out=XP[:, 0:4, :], in_=src[:, 0:4, :]), e.dma_start(out=XP[:, 4:8, :], in_=src[:, 4:8, :])],
                      reads=[ARENA], writes=[XPB], dma="xin", ndma=2)
                hook = lambda: rmsnorm_to_xn(0, src=XP, srcB=XPB, split=True)
            if "f2" in PH:
                load_wd("d2")
                ffn("g2", "u2", "d2", 2, mid_hook=hook)
            sqB = bb("SQ")
            for c in range(8):
                S.add("act", lambda e, c=c: e.activation(SQ[:, c, :], xT[:, c, :], AF.Square), reads=[XT, ARENA], writes=[sqB])
            mm_group(bank(6), [(onesb[:], SQ[:, c, :]) for c in range(8)], [CONST, sqB, ARENA], bankB[6])
            S.add("act", lambda e: e.activation(rstd[:], bank(6), AF.Sqrt, scale=1.0 / D, bias=RMS_EPS), reads=[bankB[6]], writes=[RSTD])
            S.add("dve", lambda e: e.reciprocal(rstd[:], rstd[:]), reads=[RSTD], writes=[RSTD])
            o = PP_COLS["norms"][0]
            for c in range(8):
                gcol = PP[:, o + 3 * 8 + c: o + 3 * 8 + c + 1]
                S.add("dve", lambda e, c=c, gcol=gcol: e.scalar_tensor_tensor(out=outT[:, c, :], in0=xT[:, c, :], scalar=gcol, in1=rstd[:], op0=ALU.mult, op1=ALU.mult),
                      reads=[XT, RSTD, PPB, ARENA], writes=[OUTB])
            dst = out_d.rearrange("(c p) t -> p c t", p=128)[:, :, ts]
            op = S.add("sp", lambda e, dst=dst: [e.dma_start(out=dst[:, 0:4, :], in_=outT[:, 0:4, :]), e.dma_start(out=dst[:, 4:8, :], in_=outT[:, 4:8, :])],
                       reads=[OUTB, ARENA], dma="oout", ndma=2)
            finals.append(op.idx)
        S.emit(final_waits=finals)
    return nc


_NC_CACHE = {}


def kernel(**inputs):
    NT = int(os.environ.get("KNT", SEQ // TT))
    T = NT * TT
    x = np.asarray(inputs["x"], np.float32)
    nb = x.shape[0]
    if NT not in _NC_CACHE:
        _NC_CACHE[NT] = build(NT)
    nc = _NC_CACHE[NT]
    pp = _pack_params(inputs)
    shared = {
        "ppin": pp,
        "w_g1": np.ascontiguousarray(inputs["ffn1_w_gate"][0], np.float32),
        "w_u1": np.ascontiguousarray(inputs["ffn1_w_up"][0], np.float32),
        "w_d1": np.ascontiguousarray(inputs["ffn1_w_down"][0], np.float32),
        "w_win": np.ascontiguousarray(inputs["w_in"][0], np.float32),
        "w_wout": np.ascontiguousarray(inputs["w_out"][0], np.float32),
        "w_g2": np.ascontiguousarray(inputs["ffn2_w_gate"][0], np.float32),
        "w_u2": np.ascontiguousarray(inputs["ffn2_w_up"][0], np.float32),
        "w_d2": np.ascontiguousarray(inputs["ffn2_w_down"][0], np.float32),
    }
    in_maps = []
    for b in range(nb):
        m = dict(shared)
        m["xTin"] = np.ascontiguousarray(x[b, :T, :].T)
        in_maps.append(m)
    res = run_bass_kernel_spmd(nc, in_maps, core_ids=list(range(nb)))
    out = np.stack([np.ascontiguousarray(res.results[b]["outTd"].T) for b in range(nb)], axis=0)
    return out.astype(np.float32)
```

```python
import os
import contextlib
import numpy as np
import concourse.bass as bass
import concourse.mybir as mybir
from concourse.bass_utils import run_bass_kernel_spmd

F32 = mybir.dt.float32
BF16 = mybir.dt.bfloat16
ALU = mybir.AluOpType
AF = mybir.ActivationFunctionType
AX = mybir.AxisListType

D = 1024
SEQ = 8192
DFF = 2816
NF = DFF // 128
TT = 512
C = 64
NCH = TT // C
INC = 2208
C0 = float(np.exp(-0.5))
RMS_EPS = 1e-6
GN_EPS = 64e-5


class Buf:
    __slots__ = ("name", "w", "r", "psum")

    def __init__(self, name, psum=False):
        self.name = name
        self.w = None
        self.r = []
        self.psum = psum


class Op:
    __slots__ = ("eng", "fn", "deps", "idx", "inc", "semval", "dma", "ndma")

    def __init__(self, eng, fn, dma=None, ndma=1):
        self.eng = eng
        self.fn = fn
        self.deps = set()
        self.inc = False
        self.semval = None
        self.dma = dma
        self.ndma = ndma


class Sched:
    ENGS = ("pe", "act", "dve", "pool", "sp")

    def __init__(self, nc):
        self.nc = nc
        self.ops = []

    def add(self, eng, fn, reads=(), writes=(), dma=None, ndma=1):
        op = Op(eng, fn, dma, ndma)
        op.idx = len(self.ops)
        xr = [b for b in reads if b.psum]
        if xr:
            reads = [b for b in reads if not b.psum]
            writes = list(writes) + xr
        for b in reads:
            if b.w is not None:
                op.deps.add(b.w)
        for b in writes:
            if b.w is not None:
                op.deps.add(b.w)
            for r in b.r:
                op.deps.add(r)
        for b in reads:
            b.r.append(op.idx)
        for b in writes:
            b.w = op.idx
            b.r = []
        op.deps.discard(op.idx)
        self.ops.append(op)
        return op

    def emit(self, final_waits=()):
        nc = self.nc
        ops = self.ops
        pos = {}
        cnt = {}
        dma_groups = []
        for op in ops:
            if op.dma is not None:
                key = ("dma", op.dma)
                if key not in cnt:
                    cnt[key] = 0
                    dma_groups.append(op.dma)
                cnt[key] += 16 * op.ndma
            else:
                key = ("eng", op.eng)
                cnt[key] = cnt.get(key, 0) + 1
            pos[op.idx] = (key, cnt[key])
        know = {e: {} for e in self.ENGS}
        front = [None] * len(ops)
        wdeps = [None] * len(ops)
        for op in ops:
            K = know[op.eng]
            need = {}
            for d in op.deps:
                key, p = pos[d]
                if need.get(key, (0, None))[0] < p:
                    need[key] = (p, d)
            wl = []
            for key, (p, d) in sorted(need.items(), key=lambda kv: -kv[1][1]):
                if K.get(key, 0) < p:
                    wl.append(d)
                    ops[d].inc = True
                    for k2, v2 in front[d].items():
                        if K.get(k2, 0) < v2:
                            K[k2] = v2
            wdeps[op.idx] = wl
            f = dict(K)
            k0, v0 = pos[op.idx]
            if f.get(k0, 0) < v0:
                f[k0] = v0
            front[op.idx] = f
        for i in final_waits:
            ops[i].inc = True
        counters = {}
        for op in ops:
            if op.dma is not None:
                op.semval = pos[op.idx]
            elif op.inc:
                key = ("eng", op.eng)
                counters[key] = counters.get(key, 0) + 1
                op.semval = (key, counters[key])
        waits = [[ops[d].semval for d in wl] for wl in wdeps]
        with contextlib.ExitStack() as st:
            sems = {}
            for e in ("pe", "act", "dve", "pool"):
                sems[("eng", e)] = st.enter_context(nc.semaphore("s_" + e))
            for g in dma_groups:
                sems[("dma", g)] = st.enter_context(nc.semaphore("d_" + g))
            block = st.enter_context(nc.Block())
            per_eng = {e: [op for op in ops if op.eng == e] for e in self.ENGS}

            def run(engobj, elist, ename):
                waited = {}
                for op in elist:
                    todo = []
                    for key, val in waits[op.idx]:
                        if waited.get(key, 0) < val:
                            todo.append((key, val))
                            waited[key] = val
                    attach = None
                    if todo and op.dma is None and ename in ("act", "dve", "pool"):
                        attach = todo.pop()
                    for key, val in todo:
                        engobj.wait_ge(sems[key], val)
                    res = op.fn(engobj)
                    if attach is not None:
                        assert not isinstance(res, (list, tuple))
                        res._wait_ge(sems[attach[0]], attach[1])
                    if op.dma is not None:
                        if not isinstance(res, (list, tuple)):
                            res = [res]
                        assert len(res) == op.ndma, (len(res), op.ndma)
                        for r in res:
                            r.then_inc(sems[op.semval[0]], 16)
                    elif op.inc:
                        if isinstance(res, (list, tuple)):
                            res = res[-1]
                        res.then_inc(sems[op.semval[0]], 1)
                if ename == "sp":
                    need = {}
                    for i in final_waits:
                        key, val = ops[i].semval
                        need[key] = max(need.get(key, 0), val)
                    for key, val in need.items():
                        if waited.get(key, 0) < val:
                            engobj.wait_ge(sems[key], val)
                            waited[key] = val

            @block.tensor
            def _(e):
                run(e, per_eng["pe"], "pe")

            @block.scalar
            def _(e):
                run(e, per_eng["act"], "act")

            @block.vector
            def _(e):
                run(e, per_eng["dve"], "dve")

            @block.gpsimd
            def _(e):
                run(e, per_eng["pool"], "pool")

            @block.sync
            def _(e):
                run(e, per_eng["sp"], "sp")


PP_COLS = {}
_pp_off = 0


def _pp(name, n):
    global _pp_off
    PP_COLS[name] = (_pp_off, n)
    _pp_off += n


_pp("norms", 32)
_pp("mu", 15)
_pp("w0", 4)
_pp("a0", 4)
_pp("kk", 4)
_pp("ka", 4)
_pp("rk", 4)
_pp("psc", 4)
_pp("lnwp", 4)
_pp("lnbp", 4)
_pp("ident", 128)
_pp("blk", 128)
_pp("ones", 128)
_pp("sel", 2)
_pp("maskkr", 256)
_pp("maskl", 128)
_pp("fix", 64)
PP_KEEP = _pp_off
_pp("scanmask", 512)
_pp("wup", 512)
_pp("aup", 512)
_pp("gup", 512)
_pp("wpool", 512)
PP_N = _pp_off


def _pack_params(inp):
    pp = np.zeros((128, PP_N), np.float32)

    def put(name, arr):
        o, n = PP_COLS[name]
        arr = np.asarray(arr, np.float32)
        pp[:arr.shape[0], o:o + arr.shape[1]] = arr

    def pc(v, nchunk):
        return np.asarray(v, np.float32).reshape(nchunk, 128).T

    norms = np.stack([pc(inp["ffn1_norm"][0], 8), pc(inp["mix_norm"][0], 8),
                      pc(inp["ffn2_norm"][0], 8), pc(inp["final_norm"], 8)], axis=1)
    put("norms", norms.reshape(128, 32))
    mu = np.asarray(inp["mu_shift"][0], np.float32)
    mut = np.zeros((128, 15), np.float32)
    mut[:, 0:12] = mu[0:1536].reshape(12, 128).T
    mut[0:32, 12] = mu[1536:1568]
    mut[0:32, 13] = mu[1568:1600]
    mut[0:96, 14] = mu[1600:1696]
    put("mu", mut)
    put("w0", pc(inp["w0"][0], 4))
    put("a0", pc(inp["a0"][0], 4))
    put("kk", pc(inp["k_k"][0], 4))
    put("ka", pc(inp["k_a"][0], 4))
    put("rk", pc(np.asarray(inp["r_k"][0]).reshape(512), 4))
    put("psc", pc(inp["pool_scale"][0], 4))
    put("lnwp", pc(inp["ln_w"][0], 4))
    put("lnbp", pc(inp["ln_b"][0], 4))
    put("wup", inp["w_lora_up"][0])
    put("aup", inp["a_lora_up"][0])
    put("gup", inp["g_lora_up"][0])
    put("wpool", np.transpose(np.asarray(inp["w_pool"][0], np.float32), (1, 0, 2)).reshape(128, 512))
    put("ident", np.eye(128, dtype=np.float32))
    blk = np.zeros((128, 128), np.float32)
    blk[0:64, 0:64] = 1
    blk[64:, 64:] = 1
    put("blk", blk)
    put("ones", np.ones((128, 128), np.float32))
    sel = np.zeros((128, 2), np.float32)
    sel[0:64, 0] = 1
    sel[64:, 1] = 1
    put("sel", sel)
    su = np.triu(np.ones((64, 64), np.float32), 1)
    ui = np.triu(np.ones((64, 64), np.float32), 0)
    def bd(m):
        z = np.zeros((128, 128), np.float32)
        z[0:64, 0:64] = m
        z[64:, 64:] = m
        return z
    put("maskkr", np.concatenate([bd(su), bd(ui)], 1))
    put("maskl", bd(su.T))
    sm = np.ones((128, 512), np.float32)
    sm[:, 0::64] = 0
    put("scanmask", sm)
    fix = np.ones((128, 4, 16), np.float32)
    for g, win in enumerate((2, 4, 8, 16)):
        t = np.arange(16)
        fix[:, g, :] = win / np.minimum(t + 1, win)
    put("fix", fix.reshape(128, 64))
    return pp


def build(NT):
    nc = bass.Bass("TRN2", target_bir_lowering=False)
    T = NT * TT
    dt_in = lambda name, shape: nc.dram_tensor(name, shape, F32, kind="ExternalInput").ap()
    xT_d = dt_in("xTin", [D, T])
    pp_d = dt_in("ppin", [128, PP_N])
    wsrc = {
        "g1": dt_in("w_g1", [D, DFF]), "u1": dt_in("w_u1", [D, DFF]), "d1": dt_in("w_d1", [DFF, D]),
        "win": dt_in("w_win", [D, INC]), "wout": dt_in("w_wout", [D, D]),
        "g2": dt_in("w_g2", [D, DFF]), "u2": dt_in("w_u2", [D, DFF]), "d2": dt_in("w_d2", [DFF, D]),
    }
    out_d = nc.dram_tensor("outTd", [D, T], F32, kind="ExternalOutput").ap()
    wbf = {k: nc.dram_tensor("bf_" + k, list(v.shape), BF16, kind="Internal").ap() for k, v in wsrc.items()}

    S = Sched(nc)
    st = contextlib.ExitStack()
    with st:
        sb = lambda name, shape, dt: st.enter_context(nc.sbuf_tensor(name, shape, dt))
        PP = sb("PP", [128, PP_KEEP], F32)
        xT = sb("xT", [128, 8, TT], F32)
        xn = sb("xn", [128, 8, TT], BF16)
        RING_N = 3
        ring = [sb("ring%d" % i, [128, 8, 512], BF16) for i in range(RING_N)]
        WD = sb("WD", [128, NF, D], BF16)
        rstd = sb("rstd", [128, TT], F32)
        identb = sb("identb", [128, 128], BF16)
        blkb = sb("blkb", [128, 128], BF16)
        onesb = sb("onesb", [128, 128], BF16)
        scanb = sb("scanb", [128, 512], BF16)
        wupb = sb("wupb", [32, 512], BF16)
        aupb = sb("aupb", [32, 512], BF16)
        gupb = sb("gupb", [96, 512], BF16)
        wpoolb = sb("wpoolb", [128, 512], BF16)
        omm = sb("omm", [128, 15], F32)
        omka = sb("omka", [128, 4], F32)
        carry = sb("carry", [128, 15], F32)
        halo = sb("halo", [128, 4, 16], F32)
        Hbd = sb("Hbd", [128, 4, 64], F32)
        Hbf = sb("Hbf", [128, 4, 64], BF16)
        Wc = sb("Wc", [128, 4, NCH], F32)
        small = sb("small", [128, 32], F32)
        dummy = sb("dummyt", [128, 1], F32)
        ARENA_BYTES = 102 * 1024
        arena = sb("arena", [128, ARENA_BYTES // 4], F32)
        arena_bf = arena.bitcast(BF16)

        def av(off_bytes, shape, dt):
            n = int(np.prod(shape[1:]))
            if dt == F32:
                assert off_bytes % 4 == 0
                base = arena[0:shape[0], off_bytes // 4: off_bytes // 4 + n]
            else:
                assert off_bytes % 2 == 0
                base = arena_bf[0:shape[0], off_bytes // 2: off_bytes // 2 + n]
            if len(shape) == 3:
                base = base.rearrange("p (a b) -> p a b", b=shape[2])
            return base

        KB = 1024
        Hff = av(0, [128, NF, TT], BF16)
        outT = av(22 * KB, [128, 8, TT], F32)
        SQ = av(88 * KB, [128, 8, TT], BF16)
        SG = [av(96 * KB + i * KB, [128, TT], BF16) for i in range(2)]
        Rf = av(0, [128, 4, TT], F32)
        YM = av(0, [128, 8, TT], BF16)
        PKf = av(8 * KB, [128, 4, TT], F32)
        VFb = av(16 * KB, [128, 4, TT], BF16)
        POOLP = av(20 * KB, [128, 4, 528], F32)
        TANHPW = av(28 * KB + 512, [32, TT], BF16)
        PAb = av(29 * KB + 512, [32, TT], BF16)
        SIGPG = av(30 * KB + 512, [96, TT], BF16)
        VT2 = av(32 * KB, [128, 4 * NCH, 64], BF16)
        KBTbd = av(36 * KB, [128, 4 * NCH, 128], BF16)
        BBTbd = av(44 * KB, [128, 4 * NCH, 128], BF16)
        QRbd = av(52 * KB, [128, 4 * NCH, 256], BF16)
        KTbd = av(68 * KB, [128, 4 * NCH, 128], BF16)
        BTbd = av(76 * KB, [128, 4 * NCH, 128], BF16)
        BONF = av(84 * KB, [128, 4, TT], BF16)
        tF = [av(88 * KB + i * 2 * KB, [128, TT], F32) for i in range(7)]
        praw = av(88 * KB, [128, 520], F32)
        tl = av(92 * KB, [128, TT], F32)
        KBbd = av(88 * KB, [128, NCH, 128], BF16)
        BBbd = av(90 * KB, [128, NCH, 128], BF16)
        Vbd = av(94 * KB, [128, NCH, 128], BF16)
        SQK = av(100 * KB, [128, TT], BF16)
        RKt = av(101 * KB, [128, TT], BF16)
        PBtmp = av(100 * KB, [128, TT], BF16)
        AKm = av(8 * KB, [128, 4, 256], BF16)
        ABm = av(10 * KB, [128, 4, 256], BF16)
        Mmb = [av(12 * KB + i * KB, [128, 4, 128], BF16) for i in range(2)]
        NTb = [av(14 * KB + i * 2 * KB, [128, 4, 256], BF16) for i in range(2)]
        Hdec = av(88 * KB, [128, 4, 64], F32)
        Ysb = av(89 * KB, [128, 4, 64], F32)
        tmpA = av(90 * KB, [128, 4, 64], F32)
        tmpB = av(91 * KB, [128, 4, 64], F32)
        ynbd = av(92 * KB, [128, 4, 128], BF16)
        Xb = av(93 * KB, [128, 4, 64], BF16)
        NUb = av(93 * KB + 512, [128, 4, 64], BF16)

        PS = st.enter_context(nc.psum_tensor("PS", [128, 8 * 512], F32))
        PSbf = PS.bitcast(BF16)
        bankB = [Buf("bank%d" % i, psum=True) for i in range(8)]

        def bank(i):
            return PS[:, i * 512:(i + 1) * 512]

        def bankbf(i):
            return PSbf[:, i * 1024:(i + 1) * 1024]

        B = {}

        def bb(name):
            if name not in B:
                B[name] = Buf(name)
            return B[name]

        ARENA = bb("ARENA")

        def barrier():
            S.add("pool", lambda e: e.memset(dummy[:], 0.0), writes=[ARENA, bb("dummy")])

        def ppv(name, rows=128):
            o, n = PP_COLS[name]
            return PP[0:rows, o:o + n]

        S.add("sp", lambda e: e.dma_start(out=PP[:], in_=pp_d[:, 0:PP_KEEP]), writes=[bb("PP")], dma="pp")
        STG = arena[:, 0:PP_N - PP_KEEP]
        S.add("sp", lambda e: e.dma_start(out=STG, in_=pp_d[:, PP_KEEP:PP_N]), writes=[bb("STG")], dma="pp2")

        def stg(name, rows=128):
            o, n = PP_COLS[name]
            return STG[0:rows, o - PP_KEEP:o - PP_KEEP + n]
        cvt_groups = {"c1": ["g1", "u1", "d1"], "c2": ["win", "wout"], "c3": ["g2", "u2", "d2"]}
        cvtB = {}
        for grp, names in cvt_groups.items():
            fns = []
            for nm in names:
                rows = wsrc[nm].shape[0]
                for r0 in range(0, rows, 128):
                    fns.append((nm, r0))

            def f(e, fns=fns):
                return [e.dma_start(out=wbf[nm][r0:r0 + 128, :], in_=wsrc[nm][r0:r0 + 128, :]) for nm, r0 in fns]
            for nm in names:
                cvtB[nm] = bb("cvt_" + grp)
            S.add("pool", f, writes=[bb("cvt_" + grp)], dma="cvt_" + grp, ndma=len(fns))

        cp = lambda eng, o, i, reads, writes: S.add(eng, lambda e: e.tensor_copy(o, i), reads=reads, writes=writes)
        cp("dve", identb[:], ppv("ident"), [bb("PP")], [bb("consts")])
        cp("dve", blkb[:], ppv("blk"), [bb("PP")], [bb("consts")])
        cp("dve", onesb[:], ppv("ones"), [bb("PP")], [bb("consts")])
        cp("dve", scanb[:], stg("scanmask"), [bb("STG")], [bb("consts")])
        cp("dve", wupb[:], stg("wup", 32), [bb("STG")], [bb("consts")])
        cp("dve", aupb[:], stg("aup", 32), [bb("STG")], [bb("consts")])
        cp("dve", gupb[:], stg("gup", 96), [bb("STG")], [bb("consts")])
        cp("dve", wpoolb[:], stg("wpool"), [bb("STG")], [bb("consts")])
        S.add("pool", lambda e: e.memset(dummy[:], 0.0), reads=[bb("consts")], writes=[ARENA, bb("dummy")])
        S.add("dve", lambda e: e.tensor_scalar(out=omm[:], in0=ppv("mu"), scalar1=-1.0, scalar2=1.0, op0=ALU.mult, op1=ALU.add),
              reads=[bb("PP")], writes=[bb("consts")])
        S.add("dve", lambda e: e.tensor_scalar(out=omka[:], in0=ppv("ka"), scalar1=-1.0, scalar2=1.0, op0=ALU.mult, op1=ALU.add),
              reads=[bb("PP")], writes=[bb("consts")])
        S.add("pool", lambda e: e.memset(carry[:], 0.0), writes=[bb("carry")])
        S.add("pool", lambda e: e.memset(halo[:], 0.0), writes=[bb("halo")])
        S.add("pool", lambda e: e.memset(Hbd[:], 0.0), writes=[bb("Hbd")])
        S.add("pool", lambda e: e.memset(Hbf[:], 0.0), writes=[bb("Hbf")])
        CONST = bb("consts")
        PPB = bb("PP")

        def unit_specs_ffn(gk, uk):
            sp = []
            for j in range(6):
                c0 = j * 512
                w = min(512, DFF - c0)
                sp.append((gk, c0, w))
                sp.append((uk, c0, w))
            return sp

        PH = os.environ.get("KPH", "f1,mix,f2").split(",")
        tile_specs = ((unit_specs_ffn("g1", "u1") if "f1" in PH else [])
                      + ([("win", 0, 512), ("win", 512, 512), ("win", 1024, 512), ("win", 1536, 160), ("win", 1696, 512),
                         ("wout", 0, 512), ("wout", 512, 512)] if "mix" in PH else [])
                      + (unit_specs_ffn("g2", "u2") if "f2" in PH else []))
        all_specs = tile_specs * NT
        ringB = [Buf("ring%d" % i) for i in range(RING_N)]
        wstate = {"issued": 0, "next": 0}

        def issue_load(n):
            nm, c0, w = all_specs[n]
            s = n % RING_N
            src = wbf[nm].rearrange("(c p) n -> p c n", p=128)[:, :, c0:c0 + w]
            S.add("sp", lambda e, s=s, src=src, w=w: e.dma_start(out=ring[s][:, :, 0:w], in_=src),
                  reads=[cvtB[nm]], writes=[ringB[s]], dma="ring%d" % s)

        def wget(expect):
            n = wstate["next"]
            assert all_specs[n][0] == expect, (all_specs[n], expect)
            while wstate["issued"] < min(len(all_specs), n + RING_N - 1):
                issue_load(wstate["issued"])
                wstate["issued"] += 1
            wstate["next"] = n + 1
            s = n % RING_N
            return ring[s], ringB[s], all_specs[n][2]

        WDB = bb("WD")

        def load_wd(nm):
            src = wbf[nm].rearrange("(f p) d -> p f d", p=128)

            def f(e):
                return [e.dma_start(out=WD[:, 0:11, :], in_=src[:, 0:11, :]),
                        e.dma_start(out=WD[:, 11:22, :], in_=src[:, 11:22, :])]
            S.add("sp", f, reads=[cvtB[nm]], writes=[WDB], dma="wd", ndma=2)

        XT = bb("xT")
        XN = bb("xn")
        RSTD = bb("rstd")

        def mm_group(out_ap, pairs, reads, obuf):
            n = len(pairs)

            def f(e):
                r = None
                for i, (l, rr) in enumerate(pairs):
                    r = e.matmul(out_ap, l, rr, start=(i == 0), stop=(i == n - 1))
                return r
            S.add("pe", f, reads=reads, writes=[obuf])

        XP = av(38 * KB, [128, 8, TT], F32)
        XPB = bb("XP")

        def rmsnorm_to_xn(ni, src=None, srcB=None, split=False):
            if src is None:
                src, srcB = xT, XT
            sqB = bb("SQ")
            for c in range(8):
                eng = "act" if c % 2 == 0 else "pool"
                if eng == "act":
                    S.add("act", lambda e, c=c: e.activation(SQ[:, c, :], src[:, c, :], AF.Square), reads=[srcB, ARENA], writes=[sqB])
                else:
                    S.add("pool", lambda e, c=c: e.tensor_tensor(out=SQ[:, c, :], in0=src[:, c, :], in1=src[:, c, :], op=ALU.mult),
                          reads=[srcB, ARENA], writes=[sqB])

            def rest():
                mm_group(bank(6), [(onesb[:], SQ[:, c, :]) for c in range(8)], [CONST, sqB, ARENA], bankB[6])
                S.add("act", lambda e: e.activation(rstd[:], bank(6), AF.Sqrt, scale=1.0 / D, bias=RMS_EPS), reads=[bankB[6]], writes=[RSTD])
                S.add("dve", lambda e: e.reciprocal(rstd[:], rstd[:]), reads=[RSTD], writes=[RSTD])
                o, _ = PP_COLS["norms"]
                for c in range(8):
                    gcol = PP[:, o + ni * 8 + c: o + ni * 8 + c + 1]
                    S.add("dve", lambda e, c=c, gcol=gcol: e.scalar_tensor_tensor(out=xn[:, c, :], in0=src[:, c, :], scalar=gcol, in1=rstd[:],
                                                                              op0=ALU.mult, op1=ALU.mult),
                          reads=[srcB, RSTD, PPB, ARENA], writes=[XN])
            if split:
                return rest
            rest()

        def ffn(gk, uk, dk, ni, skip_norm=False, mid_hook=None):
            HB = bb("Hff")
            if not skip_norm:
                rmsnorm_to_xn(ni)
            f = 0
            for j in range(6):
                gt, gB, w = wget(gk)
                ut, uB, _ = wget(uk)
                for jj in range(w // 128):
                    gb = f % 2
                    ub = 2 + f % 2
                    mm_group(bank(gb), [(gt[:, c, jj * 128:(jj + 1) * 128], xn[:, c, :]) for c in range(8)], [gB, XN], bankB[gb])
                    mm_group(bank(ub), [(ut[:, c, jj * 128:(jj + 1) * 128], xn[:, c, :]) for c in range(8)], [uB, XN], bankB[ub])
                    sgB = bb("SG%d" % (f % 2))
                    S.add("act", lambda e, gb=gb, f=f: e.activation(SG[f % 2][:], bank(gb), AF.Silu), reads=[bankB[gb], ARENA], writes=[sgB])
                    S.add("dve", lambda e, ub=ub, f=f: e.tensor_tensor(out=Hff[:, f, :], in0=bank(ub), in1=SG[f % 2][:], op=ALU.mult),
                          reads=[bankB[ub], sgB, ARENA], writes=[HB])
                    f += 1
            late = mid_hook() if mid_hook is not None else None
            for d in range(8):
                ob = 4 + d % 2
                mm_group(bank(ob), [(WD[:, ff, d * 128:(d + 1) * 128], Hff[:, ff, :]) for ff in range(NF)], [WDB, HB, ARENA], bankB[ob])
                S.add("dve", lambda e, d=d, ob=ob: e.scalar_tensor_tensor(out=xT[:, d, :], in0=bank(ob), scalar=0.5, in1=xT[:, d, :],
                                                                      op0=ALU.mult, op1=ALU.add),
                      reads=[bankB[ob], XT], writes=[XT])
                if d == 3 and late is not None:
                    late()

        mu_o = PP_COLS["mu"][0]

        def lerp_evict(psb, rows, idx, dest, destB, dest_reads=()):
            prB = bb("praw")
            muc = PP[0:rows, mu_o + idx:mu_o + idx + 1]
            S.add("act", lambda e: e.activation(praw[0:rows, 1:513], bank(psb)[0:rows, :], AF.Copy, scale=muc), reads=[bankB[psb], PPB, ARENA], writes=[prB])
            S.add("act", lambda e: e.activation(tl[0:rows, :], bank(psb)[0:rows, :], AF.Copy, scale=omm[0:rows, idx:idx + 1]), reads=[bankB[psb], CONST, ARENA], writes=[bb("tl")])
            S.add("pool", lambda e: e.tensor_copy(praw[0:rows, 0:1], carry[0:rows, idx:idx + 1]), reads=[bb("carry"), ARENA], writes=[prB])
            S.add("dve", lambda e: e.tensor_tensor(out=dest, in0=praw[0:rows, 0:512], in1=tl[0:rows, :], op=ALU.add),
                  reads=[prB, bb("tl"), ARENA] + list(dest_reads), writes=[destB])
            S.add("pool", lambda e: e.tensor_copy(carry[0:rows, idx:idx + 1], praw[0:rows, 512:513]), reads=[prB, ARENA], writes=[bb("carry")])

        KCUT = int(os.environ.get("KCUT", "9"))
        KC2 = int(os.environ.get("KC2", "9"))

        def mixer_tail():
            wget("wout")
            wget("wout")
            barrier()

        def mixer(ti):
            rmsnorm_to_xn(1)
            barrier()
            RB, PKB, VFB = bb("Rf"), bb("PKf"), bb("VFb")
            pbank = [0]

            def nb():
                pbank[0] = (pbank[0] + 1) % 4
                return pbank[0]
            for qi, (dest3, dB) in enumerate(((Rf, RB), (PKf, PKB), (VFb, VFB))):
                wt, wB, _ = wget("win")
                for g in range(4):
                    b_ = nb()
                    mm_group(bank(b_), [(wt[:, c, g * 128:(g + 1) * 128], xn[:, c, :]) for c in range(8)], [wB, XN], bankB[b_])
                    lerp_evict(b_, 128, qi * 4 + g, dest3[:, g, :], dB)
            wt, wB, _ = wget("win")
            LB = bb("lora")
            for li, (c0, rows, idx, dst, func) in enumerate(((0, 32, 12, TANHPW, AF.Tanh), (32, 32, 13, PAb, AF.Copy), (64, 96, 14, SIGPG, AF.Sigmoid))):
                b_ = nb()
                mm_group(bank(b_)[0:rows, :], [(wt[:, c, c0:c0 + rows], xn[:, c, :]) for c in range(8)], [wB, XN], bankB[b_])
                lt = tF[3]
                lerp_evict(b_, rows, idx, lt[0:rows, :], bb("tF3"))
                S.add("act", lambda e, dst=dst, rows=rows, func=func, lt=lt: e.activation(dst[0:rows, :], lt[0:rows, :], func),
                      reads=[bb("tF3"), ARENA], writes=[LB])
            wt, wB, _ = wget("win")
            PPOOL = bb("POOLP")
            for g in range(4):
                b_ = nb()
                mm_group(bank(b_), [(wt[:, c, g * 128:(g + 1) * 128], xn[:, c, :]) for c in range(8)], [wB, XN], bankB[b_])
                S.add("act", lambda e, g=g, b_=b_: e.activation(POOLP[:, g, 16:528], bank(b_), AF.Copy), reads=[bankB[b_], ARENA], writes=[PPOOL])
            S.add("pool", lambda e: e.tensor_copy(POOLP[:, :, 0:16], halo[:]), reads=[bb("halo"), ARENA], writes=[PPOOL])

            barrier()
            if KCUT <= 1:
                return mixer_tail()
            tB = [bb("tF%d" % i) for i in range(7)]
            CH = bb("chunkops")
            v3 = lambda ap: ap.rearrange("p (c t) -> p c t", t=C)

            sel_o = PP_COLS["sel"][0]

            def bdmul(dst, blk0, col0, in0, in1, reads, writes):
                for par in range(2):
                    mcol = PP[:, sel_o + par: sel_o + par + 1]
                    o = dst[:, blk0:blk0 + NCH, col0 + par * 64: col0 + par * 64 + 64]
                    if in1 is None:
                        S.add("act", lambda e, o=o, mcol=mcol: e.activation(o, v3(in0), AF.Copy, scale=mcol),
                              reads=list(reads) + [PPB], writes=writes)
                    else:
                        S.add("dve", lambda e, o=o, mcol=mcol: e.scalar_tensor_tensor(out=o, in0=v3(in0), scalar=mcol, in1=v3(in1), op0=ALU.mult, op1=ALU.mult),
                              reads=list(reads) + [PPB], writes=writes)

            for g in range(4):
                lam, a_, L_, kk_, nrm, k5, b6 = tF
                lamB, aB, LB_, kkB, nrmB, k5B, b6B = tB
                E_, EB = nrm, nrmB
                w0c = PP[:, PP_COLS["w0"][0] + g: PP_COLS["w0"][0] + g + 1]
                a0c = PP[:, PP_COLS["a0"][0] + g: PP_COLS["a0"][0] + g + 1]
                kkc = PP[:, PP_COLS["kk"][0] + g: PP_COLS["kk"][0] + g + 1]
                kac = PP[:, PP_COLS["ka"][0] + g: PP_COLS["ka"][0] + g + 1]
                rkc = PP[:, PP_COLS["rk"][0] + g: PP_COLS["rk"][0] + g + 1]
                gs = slice(g * 128, (g + 1) * 128)
                g8 = g * NCH
                mm_group(bank(4), [(wupb[:, gs], TANHPW[:, :])], [CONST, LB, ARENA], bankB[4])
                S.add("act", lambda e, w0c=w0c: e.activation(lam[:], bank(4), AF.Sigmoid, bias=w0c), reads=[bankB[4], PPB, ARENA], writes=[lamB])
                mm_group(bank(5), [(aupb[:, gs], PAb[:, :])], [CONST, LB, ARENA], bankB[5])
                S.add("act", lambda e, a0c=a0c: e.activation(a_[:], bank(5), AF.Sigmoid, bias=a0c), reads=[bankB[5], PPB, ARENA], writes=[aB])
                S.add("dve", lambda e: e.tensor_tensor_scan(out=L_[:], data0=scanb[:], data1=lam[:], initial=0.0, op0=ALU.mult, op1=ALU.add),
                      reads=[lamB, CONST, ARENA], writes=[LB_])
                S.add("act", lambda e, g=g, kkc=kkc: e.activation(kk_[:], PKf[:, g, :], AF.Copy, scale=kkc),
                      reads=[PKB, PPB, ARENA], writes=[kkB])
                S.add("act", lambda e: e.activation(SQK[:], kk_[:], AF.Square), reads=[kkB, ARENA], writes=[b6B])
                mm_group(bank(6), [(blkb[:], SQK[:])], [CONST, b6B, ARENA], bankB[6])
                S.add("act", lambda e: e.activation(nrm[:], bank(6), AF.Sqrt), reads=[bankB[6], ARENA], writes=[nrmB])
                S.add("dve", lambda e: e.tensor_scalar(out=nrm[:], in0=nrm[:], scalar1=1e-12, scalar2=None, op0=ALU.max), reads=[nrmB, ARENA], writes=[nrmB])
                S.add("dve", lambda e: e.reciprocal(nrm[:], nrm[:]), reads=[nrmB, ARENA], writes=[nrmB])
                S.add("dve", lambda e: e.tensor_tensor(out=kk_[:], in0=kk_[:], in1=nrm[:], op=ALU.mult), reads=[kkB, nrmB, ARENA], writes=[kkB])
                S.add("act", lambda e, g=g, kac=kac: e.activation(k5[:], a_[:], AF.Identity, scale=kac, bias=omka[:, g:g + 1]),
                      reads=[aB, PPB, CONST, ARENA], writes=[k5B])
                S.add("dve", lambda e, g=g: e.tensor_tensor(out=k5[:], in0=k5[:], in1=PKf[:, g, :], op=ALU.mult), reads=[k5B, PKB, ARENA], writes=[k5B])
                S.add("dve", lambda e, g=g, rkc=rkc: e.scalar_tensor_tensor(out=RKt[:], in0=Rf[:, g, :], scalar=rkc, in1=k5[:], op0=ALU.mult, op1=ALU.mult),
                      reads=[RB, k5B, PPB, ARENA], writes=[b6B])
                mm_group(bank(5), [(blkb[:], RKt[:])], [CONST, b6B, ARENA], bankB[5])
                S.add("dve", lambda e, g=g: e.tensor_tensor(out=BONF[:, g, :], in0=bank(5), in1=VFb[:, g, :], op=ALU.mult), reads=[bankB[5], VFB, ARENA], writes=[CH])
                S.add("dve", lambda e: e.tensor_tensor(out=b6[:], in0=kk_[:], in1=a_[:], op=ALU.mult), reads=[kkB, aB, ARENA], writes=[b6B])
                S.add("act", lambda e: e.activation(E_[:], L_[:], AF.Exp, scale=-C0), reads=[LB_, ARENA], writes=[EB])
                bdmul(QRbd, g8, 128, Rf[:, g, :], E_[:], [RB, EB, ARENA], [CH])
                S.add("act", lambda e: e.activation(E_[:], L_[:], AF.Exp, scale=C0), reads=[LB_, ARENA], writes=[EB])
                bdmul(KTbd, g8, 0, k5[:], E_[:], [k5B, EB, ARENA], [CH])
                bdmul(BTbd, g8, 0, b6[:], E_[:], [b6B, EB, ARENA], [CH])
                Lend = L_[:].rearrange("p (c t) -> p c t", t=C)[:, :, C - 1:C]
                S.add("act", lambda e, g=g, Lend=Lend: e.activation(Wc[:, g, :].unsqueeze(2), Lend, AF.Exp, scale=-C0), reads=[LB_, ARENA], writes=[CH])
                S.add("dve", lambda e: e.tensor_tensor(out=lam[:], in0=L_[:], in1=lam[:], op=ALU.subtract), reads=[LB_, lamB, ARENA], writes=[lamB])
                S.add("act", lambda e: e.activation(E_[:], lam[:], AF.Exp, scale=-C0), reads=[lamB, ARENA], writes=[EB])
                bdmul(QRbd, g8, 0, kk_[:], E_[:], [kkB, EB, ARENA], [CH])
                S.add("dve", lambda e, Lend=Lend: e.tensor_tensor(out=lam[:].rearrange("p (c t) -> p c t", t=C), in0=L_[:].rearrange("p (c t) -> p c t", t=C),
                                                                 in1=Lend.broadcast_to([128, NCH, C]), op=ALU.subtract),
                      reads=[LB_, lamB, ARENA], writes=[lamB])
                S.add("act", lambda e: e.activation(E_[:], lam[:], AF.Exp, scale=C0), reads=[lamB, ARENA], writes=[EB])
                bdmul(KBbd, 0, 0, k5[:], E_[:], [k5B, EB, ARENA], [lamB])
                bdmul(BBbd, 0, 0, b6[:], E_[:], [b6B, EB, ARENA], [aB])
                bdmul(Vbd, 0, 0, VFb[:, g, :], None, [VFB, ARENA], [kkB])
                for qi, (src, srcB) in enumerate(((Vbd, kkB), (KBbd, lamB), (BBbd, aB))):
                    tbk = 7 if qi % 2 == 0 else 3
                    tp = bankbf(tbk).rearrange("p (c n) -> p c n", n=128)

                    def ftr(e, src=src, tp=tp):
                        r = None
                        for c in range(NCH):
                            r = e.transpose(tp[:, c, :], src[:, c, :], identb[:])
                        return r
                    S.add("pe", ftr, reads=[srcB, CONST, ARENA], writes=[bankB[tbk]])
                    if qi == 0:
                        S.add("act", lambda e, tp=tp, g8=g8: e.activation(VT2[0:64, g8:g8 + NCH, :], tp[0:64, :, 0:64], AF.Copy), reads=[bankB[tbk], ARENA], writes=[CH])
                        S.add("dve", lambda e, tp=tp, g8=g8: e.tensor_copy(VT2[64:128, g8:g8 + NCH, :], tp[64:128, :, 64:128]), reads=[bankB[tbk], ARENA], writes=[CH])
                    elif qi == 1:
                        S.add("act", lambda e, tp=tp, g8=g8: e.activation(KBTbd[:, g8:g8 + NCH, :], tp, AF.Copy), reads=[bankB[tbk], ARENA], writes=[CH])
                    else:
                        S.add("dve", lambda e, tp=tp, g8=g8: e.tensor_copy(BBTbd[:, g8:g8 + NCH, :], tp), reads=[bankB[tbk], ARENA], writes=[CH])
            barrier()

            if KCUT <= 2:
                return mixer_tail()
            YMB = bb("YM")
            tPQ = av(96 * KB, [128, 528], F32)
            tQQ = av(8 * KB + 0, [128, 528], F32)
            PQB, QQB = bb("tPQ"), bb("tQQ")
            for g, win in enumerate((2, 4, 8, 16)):
                B0 = POOLP[:, g, :]
                bufs = [(tPQ, PQB), (tQQ, QQB)]
                cur, curB = B0, PPOOL
                sh = 1
                k_ = 0
                while sh < win:
                    dst, dstB = bufs[k_ % 2]
                    lo = 2 * sh - 1
                    S.add("pool", lambda e, dst=dst, cur=cur, lo=lo, sh=sh: e.tensor_tensor(out=dst[:, lo:528], in0=cur[:, lo:528], in1=cur[:, lo - sh:528 - sh], op=ALU.add),
                          reads=[curB, ARENA], writes=[dstB])
                    cur, curB = dst, dstB
                    sh *= 2
                    k_ += 1
                if ti == 0:
                    fo = PP_COLS["fix"][0]
                    S.add("dve", lambda e, cur=cur, g=g, fo=fo: e.tensor_tensor(out=cur[:, 16:32], in0=cur[:, 16:32], in1=PP[:, fo + g * 16: fo + (g + 1) * 16], op=ALU.mult),
                          reads=[curB, PPB, ARENA], writes=[curB])
                S.add("dve", lambda e, cur=cur, B0=B0, win=win: e.scalar_tensor_tensor(out=PBtmp[:], in0=cur[:, 16:528], scalar=1.0 / win, in1=B0[:, 16:528], op0=ALU.mult, op1=ALU.subtract),
                      reads=[curB, PPOOL, ARENA], writes=[bb("PBtmp")])
                mm_group(bank(4 + g % 2), [(wpoolb[:, g * 128:(g + 1) * 128], PBtmp[:])], [CONST, bb("PBtmp"), ARENA], bankB[4 + g % 2])
                psc = PP[:, PP_COLS["psc"][0] + g: PP_COLS["psc"][0] + g + 1]
                S.add("act", lambda e, g=g, psc=psc: e.activation(YM[:, 4 + g, :], bank(4 + g % 2), AF.Identity, scale=psc), reads=[bankB[4 + g % 2], PPB, ARENA], writes=[YMB])
                S.add("pool", lambda e, g=g: e.tensor_copy(halo[:, g, :], POOLP[:, g, 512:528]), reads=[PPOOL, ARENA], writes=[bb("halo")])

            barrier()
            if KCUT <= 3:
                return mixer_tail()
            HB_, HFB = bb("Hbd"), bb("Hbf")
            mkr_b = ppv("maskkr").unsqueeze(1).broadcast_to([128, 4, 256])
            mlo_b = ppv("maskl").unsqueeze(1).broadcast_to([128, 4, 128])
            idf_b = ppv("ident").unsqueeze(1).broadcast_to([128, 4, 128])
            AKs = [AKm, av(20 * KB, [128, 4, 256], BF16)]
            ABs = [ABm, av(22 * KB, [128, 4, 256], BF16)]
            NTs = [NTb, [av(24 * KB + i * 2 * KB, [128, 4, 256], BF16) for i in range(2)]]
            Mms = [Mmb, [av(94 * KB + i * KB, [128, 4, 128], BF16) for i in range(2)]]
            AKBs = [bb("AKm0"), bb("AKm1")]
            ABBs = [bb("ABm0"), bb("ABm1")]
            MBs = [[bb("Mm00"), bb("Mm01")], [bb("Mm10"), bb("Mm11")]]
            NBs = [[bb("NT00"), bb("NT01")], [bb("NT10"), bb("NT11")]]
            XB_, NUB = bb("Xb"), bb("NUb")
            YNB = bb("ynbd")
            S.add("pool", lambda e: e.memset(ynbd[:], 0.0), reads=[ARENA], writes=[YNB])
            psK = PS[:, 0:1024].rearrange("p (g n) -> p g n", n=256)
            psB = PS[:, 1024:2048].rearrange("p (g n) -> p g n", n=256)
            psM = bank(4).rearrange("p (g n) -> p g n", n=128)
            ps1 = PS[:, 0:1024].rearrange("p (g n) -> p g n", n=256)
            ps2 = bank(2).rearrange("p (g n) -> p g n", n=128)
            psX = bank(5)[:, 0:256].rearrange("p (g n) -> p g n", n=64)
            psU = bank(5)[:, 256:512].rearrange("p (g n) -> p g n", n=64)
            psH = bank(6)[:, 0:256].rearrange("p (g n) -> p g n", n=64)
            psY = bank(6)[:, 256:512].rearrange("p (g n) -> p g n", n=64)
            ytp = bankbf(7)[:, 0:512].rearrange("p (g n) -> p g n", n=128)

            def stageA(c):
                p = c % 2
                AK, AB, NT_, Mm_ = AKs[p], ABs[p], NTs[p], Mms[p]
                AKB, ABB, NB, MB = AKBs[p], ABBs[p], NBs[p], MBs[p]

                def fA(e):
                    r = None
                    for g in range(4):
                        i = g * NCH + c
                        e.matmul(psK[:, g, :], KTbd[:, i, :], QRbd[:, i, :], start=True, stop=True)
                        e.matmul(psB[:, g, :], BTbd[:, i, :], QRbd[:, i, :], start=True, stop=True)
                        r = e.matmul(psM[:, g, :], QRbd[:, i, 0:128], BTbd[:, i, :], start=True, stop=True)
                    return r
                S.add("pe", fA, reads=[CH, ARENA], writes=[bankB[0], bankB[1], bankB[2], bankB[3], bankB[4]])
                S.add("dve", lambda e: e.tensor_tensor(out=AK[:], in0=psK, in1=mkr_b, op=ALU.mult),
                      reads=[bankB[0], bankB[1], PPB, ARENA], writes=[AKB])
                S.add("dve", lambda e: e.tensor_tensor(out=AB[:], in0=psB, in1=mkr_b, op=ALU.mult),
                      reads=[bankB[2], bankB[3], PPB, ARENA], writes=[ABB])
                S.add("dve", lambda e: e.tensor_tensor(out=Mm_[0][:], in0=psM, in1=mlo_b, op=ALU.mult),
                      reads=[bankB[4], PPB, ARENA], writes=[MB[0]])
                S.add("dve", lambda e: e.tensor_tensor(out=NT_[1][:, :, 0:128], in0=idf_b, in1=AB[:, :, 0:128], op=ALU.subtract),
                      reads=[ABB, PPB, ARENA], writes=[NB[1]])
                yield

                def f0(e):
                    r = None
                    for g in range(4):
                        e.matmul(ps1[:, g, 128:256], Mm_[0][:, g, :], AB[:, g, 0:128], start=True, stop=True)
                        r = e.matmul(ps2[:, g, :], AB[:, g, 0:128], Mm_[0][:, g, :], start=True, stop=True)
                    return r
                S.add("pe", f0, reads=[MB[0], ABB, ARENA], writes=[bankB[0], bankB[1], bankB[2]])
                S.add("act", lambda e: e.activation(Mm_[1][:], ps2, AF.Copy), reads=[bankB[2], ARENA], writes=[MB[1]])
                S.add("act", lambda e: e.activation(NT_[1][:, :, 128:256], ps1[:, :, 128:256], AF.Copy), reads=[bankB[0], bankB[1], ARENA], writes=[NB[1]])
                yield
                cur = 1
                for j in range(5):
                    nx = 1 - cur
                    last = (j == 4)

                    def fl(e, cur=cur, last=last):
                        r = None
                        for g in range(4):
                            if last:
                                r = e.matmul(ps1[:, g, 0:128], Mm_[cur][:, g, :], NT_[cur][:, g, 0:128], start=True, stop=True)
                            else:
                                e.matmul(ps1[:, g, :], Mm_[cur][:, g, :], NT_[cur][:, g, :], start=True, stop=True)
                                r = e.matmul(ps2[:, g, :], NT_[cur][:, g, 128:256], Mm_[cur][:, g, :], start=True, stop=True)
                        return r
                    S.add("pe", fl, reads=[MB[cur], NB[cur], ARENA], writes=[bankB[0], bankB[1]] + ([] if last else [bankB[2]]))
                    S.add("dve", lambda e, cur=cur, nx=nx: e.tensor_tensor(out=NT_[nx][:, :, 0:128], in0=ps1[:, :, 0:128], in1=NT_[cur][:, :, 0:128], op=ALU.add),
                          reads=[bankB[0], bankB[1], NB[cur], ARENA], writes=[NB[nx]])
                    if not last:
                        S.add("act", lambda e, nx=nx: e.activation(Mm_[nx][:], ps2, AF.Copy), reads=[bankB[2], ARENA], writes=[MB[nx]])
                        if j < 3:
                            S.add("act", lambda e, nx=nx: e.activation(NT_[nx][:, :, 128:256], ps1[:, :, 128:256], AF.Copy), reads=[bankB[0], bankB[1], ARENA], writes=[NB[nx]])
                    cur = nx
                    yield
                assert cur == 0

            def stageB(c):
                p = c % 2
                cs = slice(c * C, (c + 1) * C)
                AK, AB = AKs[p], ABs[p]
                AKB, ABB = AKBs[p], ABBs[p]
                PT = NTs[p][0][:, :, 0:128]
                PTB = NBs[p][0]

                def fX(e):
                    r = None
                    for g in range(4):
                        i = g * NCH + c
                        e.matmul(psX[:, g, :], QRbd[:, i, 0:128], Hbf[:, g, :], start=True, stop=False)
                        r = e.matmul(psX[:, g, :], AK[:, g, 0:128], VT2[:, i, :], start=False, stop=True)
                    return r
                S.add("pe", fX, reads=[CH, HFB, AKB, ARENA], writes=[bankB[5]])
                S.add("act", lambda e: e.activation(Xb[:], psX, AF.Copy), reads=[bankB[5], ARENA], writes=[XB_])
                yield

                def fU(e):
                    r = None
                    for g in range(4):
                        r = e.matmul(psU[:, g, :], PT[:, g, :], Xb[:, g, :], start=True, stop=True)
                    return r
                S.add("pe", fU, reads=[PTB, XB_, ARENA], writes=[bankB[5]])
                S.add("act", lambda e: e.activation(NUb[:], psU, AF.Copy, scale=-1.0), reads=[bankB[5], ARENA], writes=[NUB])
                yield

                def fHY(e):
                    r = None
                    for g in range(4):
                        i = g * NCH + c
                        e.matmul(psH[:, g, :], KBTbd[:, i, :], VT2[:, i, :], start=True, stop=False, skip_group_check=True)
                        r = e.matmul(psH[:, g, :], BBTbd[:, i, :], NUb[:, g, :], start=False, stop=True, skip_group_check=True)
                    for g in range(4):
                        i = g * NCH + c
                        e.matmul(psY[:, g, :], QRbd[:, i, 128:256], Hbf[:, g, :], start=(g == 0), stop=False, skip_group_check=True)
                    for g in range(4):
                        i = g * NCH + c
                        e.matmul(psY[:, g, :], AK[:, g, 128:256], VT2[:, i, :], start=False, stop=False, skip_group_check=True)
                        r = e.matmul(psY[:, g, :], AB[:, g, 128:256], NUb[:, g, :], start=False, stop=True, skip_group_check=True)
                    return r
                S.add("pe", fHY, reads=[CH, HFB, AKB, ABB, NUB, ARENA], writes=[bankB[6]])
                YB, TA, TB_, SM = bb("Ysb"), bb("tmpA"), bb("tmpB"), bb("small")
                wcb = Wc[:, :, c:c + 1].broadcast_to([128, 4, 64])
                S.add("dve", lambda e, wcb=wcb: e.tensor_tensor(out=Hdec[:], in0=Hbd[:], in1=wcb, op=ALU.mult), reads=[HB_, CH, ARENA], writes=[bb("Hdec")])
                S.add("dve", lambda e: e.tensor_tensor(out=Hbd[:], in0=psH, in1=Hdec[:], op=ALU.add), reads=[bankB[6], bb("Hdec"), ARENA], writes=[HB_])
                S.add("act", lambda e: e.activation(Hbf[:], Hbd[:], AF.Copy), reads=[HB_], writes=[HFB])
                S.add("dve", lambda e: e.tensor_copy(Ysb[:], psY), reads=[bankB[6], ARENA], writes=[YB])
                yield
                s1, s2, mean, msq, var = small[:, 0:4], small[:, 4:8], small[:, 8:12], small[:, 12:16], small[:, 16:20]
                S.add("dve", lambda e: e.tensor_reduce(out=s1, in_=Ysb[:], axis=AX.X, op=ALU.add), reads=[YB, ARENA], writes=[SM])
                S.add("act", lambda e: e.activation(tmpA[:], Ysb[:], AF.Square), reads=[YB, ARENA], writes=[TA])
                S.add("dve", lambda e: e.tensor_reduce(out=s2, in_=tmpA[:], axis=AX.X, op=ALU.add), reads=[TA, SM, ARENA], writes=[SM])
                S.add("dve", lambda e: e.tensor_scalar(out=mean, in0=s1, scalar1=1.0 / 64, scalar2=None, op0=ALU.mult), reads=[SM], writes=[SM])
                S.add("dve", lambda e: e.tensor_tensor(out=msq, in0=mean, in1=mean, op=ALU.mult), reads=[SM], writes=[SM])
                S.add("dve", lambda e: e.scalar_tensor_tensor(out=var, in0=s2, scalar=1.0 / 64, in1=msq, op0=ALU.mult, op1=ALU.subtract), reads=[SM], writes=[SM])
                S.add("act", lambda e: e.activation(var, var, AF.Sqrt, bias=GN_EPS), reads=[SM], writes=[SM])
                S.add("dve", lambda e: e.reciprocal(var, var), reads=[SM], writes=[SM])
                mean_b = mean.unsqueeze(2).broadcast_to([128, 4, 64])
                S.add("dve", lambda e, mean_b=mean_b: e.tensor_tensor(out=tmpB[:], in0=Ysb[:], in1=mean_b, op=ALU.subtract), reads=[YB, SM, ARENA], writes=[TB_])
                for par in range(2):
                    ps_ = slice(par * 64, par * 64 + 64)
                    rb = var[ps_, :].unsqueeze(2).broadcast_to([64, 4, 64])
                    S.add("dve", lambda e, ps_=ps_, rb=rb, par=par: e.tensor_tensor(out=ynbd[ps_, :, par * 64:par * 64 + 64], in0=tmpB[ps_, :, :], in1=rb, op=ALU.mult),
                          reads=[TB_, SM, ARENA], writes=[YNB])

                def fyt(e):
                    r = None
                    for g in range(4):
                        r = e.transpose(ytp[:, g, :], ynbd[:, g, :], identb[:])
                    return r
                S.add("pe", fyt, reads=[YNB, CONST, ARENA], writes=[bankB[7]])
                S.add("act", lambda e: e.activation(YM[0:64, 0:4, cs], ytp[0:64, :, 0:64], AF.Copy), reads=[bankB[7], ARENA], writes=[YMB])
                S.add("dve", lambda e: e.tensor_copy(YM[64:128, 0:4, cs], ytp[64:128, :, 64:128]), reads=[bankB[7], ARENA], writes=[YMB])
                yield

            for _ in stageA(0):
                pass
            for c in range(NCH):
                gens = [stageB(c)] + ([stageA(c + 1)] if c + 1 < NCH else [])
                while gens:
                    for gn in list(gens):
                        try:
                            next(gn)
                        except StopIteration:
                            gens.remove(gn)

            tE, tEB = tF[4], bb("tF4")
            lw0, lb0 = PP_COLS["lnwp"][0], PP_COLS["lnbp"][0]
            for g in range(4):
                gs = slice(g * 128, (g + 1) * 128)
                gbk = 5 + g % 2
                S.add("act", lambda e, g=g: e.activation(tE[:], YM[:, g, :], AF.Identity, scale=PP[:, lw0 + g:lw0 + g + 1], bias=PP[:, lb0 + g:lb0 + g + 1]),
                      reads=[YMB, PPB, ARENA], writes=[tEB])
                S.add("dve", lambda e, g=g: e.tensor_tensor(out=tE[:], in0=tE[:], in1=BONF[:, g, :], op=ALU.add), reads=[tEB, CH, ARENA], writes=[tEB])
                mm_group(bank(gbk), [(gupb[:, gs], SIGPG[:, :])], [CONST, LB, ARENA], bankB[gbk])
                S.add("dve", lambda e, g=g, gbk=gbk: e.tensor_tensor(out=YM[:, g, :], in0=bank(gbk), in1=tE[:], op=ALU.mult),
                      reads=[bankB[gbk], tEB, ARENA], writes=[YMB])

            if KCUT <= 4:
                return mixer_tail()
            w0t, w0B, _ = wget("wout")
            w1t, w1B, _ = wget("wout")
            for d in range(8):
                wt, wB = (w0t, w0B) if d < 4 else (w1t, w1B)
                dd = d % 4
                ob = d % 2
                mm_group(bank(ob), [(wt[:, cc, dd * 128:(dd + 1) * 128], YM[:, cc, :]) for cc in range(8)], [wB, YMB, ARENA], bankB[ob])
                S.add("dve", lambda e, d=d, ob=ob: e.tensor_tensor(out=xT[:, d, :], in0=bank(ob), in1=xT[:, d, :], op=ALU.add),
                      reads=[bankB[ob], XT], writes=[XT])
            barrier()

        finals = []
        OUTB = bb("outT")
        for ti in range(NT):
            ts = slice(ti * TT, (ti + 1) * TT)
            xsrc = lambda t_: xT_d.rearrange("(c p) t -> p c t", p=128)[:, :, t_ * TT:(t_ + 1) * TT]
            FULL = ("f1" in PH and "mix" in PH and "f2" in PH)
            pre = FULL and ti > 0
            if not pre:
                src = xsrc(ti)
                S.add("sp", lambda e, src=src: [e.dma_start(out=xT[:, 0:4, :], in_=src[:, 0:4, :]), e.dma_start(out=xT[:, 4:8, :], in_=src[:, 4:8, :])],
                      writes=[XT], dma="xin", ndma=2)
            else:
                for c in range(8):
                    eng = ("dve", "pool", "act")[c % 3]
                    if eng == "act":
                        S.add("act", lambda e, c=c: e.activation(xT[:, c, :], XP[:, c, :], AF.Copy), reads=[XPB, ARENA], writes=[XT])
                    else:
                        S.add(eng, lambda e, c=c: e.tensor_copy(xT[:, c, :], XP[:, c, :]), reads=[XPB, ARENA], writes=[XT])
            if "f1" in PH:
                load_wd("d1")
                ffn("g1", "u1", "d1", 0, skip_norm=pre)
            if "mix" in PH:
                mixer(ti)
            hook = None
            if FULL and ti + 1 < NT:
                src = xsrc(ti + 1)
                S.add("sp", lambda e, src=src: [e.dma_start(out=XP[:, 0:4, :], in_=src[:, 0:4, :]), e.dma_start(out=XP[:, 4:8, :], in_=src[:, 4:8, :])],
                      reads=[ARENA], writes=[XPB], dma="xin", ndma=2)
                hook = lambda: rmsnorm_to_xn(0, src=XP, srcB=XPB, split=True)
            if "f2" in PH:
                load_wd("d2")
                ffn("g2", "u2", "d2", 2, mid_hook=hook)
            sqB = bb("SQ")
            for c in range(8):
                S.add("act", lambda e, c=c: e.activation(SQ[:, c, :], xT[:, c, :], AF.Square), reads=[XT, ARENA], writes=[sqB])
            mm_group(bank(6), [(onesb[:], SQ[:, c, :]) for c in range(8)], [CONST, sqB, ARENA], bankB[6])
            S.add("act", lambda e: e.activation(rstd[:], bank(6), AF.Sqrt, scale=1.0 / D, bias=RMS_EPS), reads=[bankB[6]], writes=[RSTD])
            S.add("dve", lambda e: e.reciprocal(rstd[:], rstd[:]), reads=[RSTD], writes=[RSTD])
            o = PP_COLS["norms"][0]
            for c in range(8):
                gcol = PP[:, o + 3 * 8 + c: o + 3 * 8 + c + 1]
                S.add("dve", lambda e, c=c, gcol=gcol: e.scalar_tensor_tensor(out=outT[:, c, :], in0=xT[:, c, :], scalar=gcol, in1=rstd[:], op0=ALU.mult, op1=ALU.mult),
                      reads=[XT, RSTD, PPB, ARENA], writes=[OUTB])
            dst = out_d.rearrange("(c p) t -> p c t", p=128)[:, :, ts]
            op = S.add("pool", lambda e, dst=dst: [e.dma_start(out=dst[:, 0:4, :], in_=outT[:, 0:4, :]), e.dma_start(out=dst[:, 4:8, :], in_=outT[:, 4:8, :])],
                       reads=[OUTB, ARENA], dma="oout", ndma=2)
            finals.append(op.idx)
        S.emit(final_waits=finals)
    return nc


_NC_CACHE = {}


def kernel(**inputs):
    NT = int(os.environ.get("KNT", SEQ // TT))
    T = NT * TT
    x = np.asarray(inputs["x"], np.float32)
    nb = x.shape[0]
    if NT not in _NC_CACHE:
        _NC_CACHE[NT] = build(NT)
    nc = _NC_CACHE[NT]
    pp = _pack_params(inputs)
    shared = {
        "ppin": pp,
        "w_g1": np.ascontiguousarray(inputs["ffn1_w_gate"][0], np.float32),
        "w_u1": np.ascontiguousarray(inputs["ffn1_w_up"][0], np.float32),
        "w_d1": np.ascontiguousarray(inputs["ffn1_w_down"][0], np.float32),
        "w_win": np.ascontiguousarray(inputs["w_in"][0], np.float32),
        "w_wout": np.ascontiguousarray(inputs["w_out"][0], np.float32),
        "w_g2": np.ascontiguousarray(inputs["ffn2_w_gate"][0], np.float32),
        "w_u2": np.ascontiguousarray(inputs["ffn2_w_up"][0], np.float32),
        "w_d2": np.ascontiguousarray(inputs["ffn2_w_down"][0], np.float32),
    }
    in_maps = []
    for b in range(nb):
        m = dict(shared)
        m["xTin"] = np.ascontiguousarray(x[b, :T, :].T)
        in_maps.append(m)
    res = run_bass_kernel_spmd(nc, in_maps, core_ids=list(range(nb)))
    out = np.stack([np.ascontiguousarray(res.results[b]["outTd"].T) for b in range(nb)], axis=0)
    return out.astype(np.float32)
```
